# Optimizing a Trainium2 kernel written in Bass

```python
import jax, jax.numpy as jnp
from jax import lax
import numpy as np

D_MODEL = 4096
BATCH = 1
SEQ = 8192
DEPTH = 1

GRID_W = 64
CTX_LEN = 256
D_MIX = D_MODEL
D_RWKV = D_MIX // 2
D_HGRN = D_MIX - D_RWKV
RWKV_HEAD = 64
RWKV_HEADS = D_RWKV // RWKV_HEAD
HGRN_EXPAND = 128
HGRN_HEADS = D_HGRN // HGRN_EXPAND
HGRN_VAL = D_HGRN // HGRN_HEADS
D_DECAY_LORA = max(32, int(round(1.8 * D_RWKV ** 0.5 / 32)) * 32)
D_AAA_LORA = max(32, int(round(1.8 * D_RWKV ** 0.5 / 32)) * 32)
D_GATE_LORA = max(32, int(round(0.6 * D_RWKV ** 0.8 / 32)) * 32)
D_FF = 4 * D_MODEL
HGRN_CHUNK = 64
N_MOD = 6
NORM_EPS = 1e-6
GN_EPS = 64e-5
RWKV_SPLITS = (D_RWKV, D_RWKV, D_RWKV, 2 * D_DECAY_LORA, 2 * D_AAA_LORA, D_GATE_LORA)
RWKV_COLS = sum(RWKV_SPLITS)
HGRN_SPLITS = (D_HGRN, 2 * D_HGRN, D_HGRN, D_HGRN)
HGRN_COLS = sum(HGRN_SPLITS)
IN_COLS = RWKV_COLS + HGRN_COLS

kernel_name = "hybrid_rwkv7_hgrn2_prefix_dit_layer"


def _split(t, sizes):
    idx = np.cumsum(sizes)[:-1].tolist()
    return jnp.split(t, idx, axis=-1)


def _rms_norm(t, g):
    tf = t.astype(jnp.float32)
    tf = tf * lax.rsqrt(jnp.mean(tf * tf, axis=-1, keepdims=True) + NORM_EPS)
    return (tf * g).astype(t.dtype)


def _modulate(h, shift, scale):
    return h * (1 + scale) + shift


def _qshift_grid(t, rows):
    B, L, C = t.shape
    t5 = t.reshape(B, rows, GRID_W, C // 4, 4)
    tp = jnp.pad(t5, ((0, 0), (1, 1), (1, 1), (0, 0), (0, 0)))
    left = tp[:, 1:-1, :-2, :, 0]
    right = tp[:, 1:-1, 2:, :, 1]
    up = tp[:, :-2, 1:-1, :, 2]
    down = tp[:, 2:, 1:-1, :, 3]
    return jnp.stack([left, right, up, down], axis=-1).reshape(B, L, C)


def _qshift_seq(t):
    B, L, C = t.shape
    t4 = t.reshape(B, L, C // 4, 4)
    tp = jnp.pad(t4, ((0, 0), (1, 1), (0, 0), (0, 0)))
    prev, nxt = tp[:, :-2], tp[:, 2:]
    return jnp.stack([prev[..., 0], nxt[..., 1], prev[..., 2], nxt[..., 3]], axis=-1).reshape(B, L, C)


def _rwkv7_features(p, shifted, mu, w0, w2, a0, a2, k_k, k_a):
    m = p + mu * (shifted - p)
    B, L, _ = m.shape
    r, k, v, wl, al, gl = _split(m, RWKV_SPLITS)
    hd = lambda t: t.reshape(B, L, RWKV_HEADS, RWKV_HEAD)
    kkf = hd(k * k_k).astype(jnp.float32)
    kk = kkf * lax.rsqrt(jnp.sum(kkf * kkf, axis=-1, keepdims=True) + 1e-12)
    wd = jnp.einsum('bldr,drc->bldc', jnp.tanh(wl.reshape(B, L, 2, D_DECAY_LORA)), w2) + w0
    log_w = -jax.nn.softplus(-wd.astype(jnp.float32)) - 0.5
    decay = jnp.exp(-jnp.exp(log_w))
    a = jax.nn.sigmoid(jnp.einsum('bldr,drc->bldc', al.reshape(B, L, 2, D_AAA_LORA), a2) + a0)
    dirs = []
    for d in range(2):
        a_d = a[:, :, d]
        k_d = k * (1 + (a_d - 1) * k_a)
        dirs.append((hd(r), hd(decay[:, :, d]), hd(k_d), hd(v), -kk, kk * hd(a_d)))
    return dirs, (hd(r), hd(k), hd(v), gl)


def _rwkv7_scan(r, w, k, v, a, b, s0):
    def step(S, inp):
        r_t, w_t, k_t, v_t, a_t, b_t = inp
        sa = jnp.einsum('bhvk,bhk->bhv', S, a_t)
        S = S * w_t[:, :, None, :] + sa[..., None] * b_t[:, :, None, :] + v_t[..., None] * k_t[:, :, None, :]
        return S, jnp.einsum('bhvk,bhk->bhv', S, r_t)
    xs = tuple(jnp.moveaxis(t.astype(jnp.float32), 1, 0) for t in (r, w, k, v, a, b))
    S, y = lax.scan(step, s0, xs)
    return jnp.moveaxis(y, 0, 1), S


def _rwkv7_out(y, r, k, v, gl, r_k, ln_g, ln_b, g2):
    B, L, H, N = y.shape
    mean = jnp.mean(y, axis=-1, keepdims=True)
    var = jnp.mean(jnp.square(y - mean), axis=-1, keepdims=True)
    yn = ((y - mean) * lax.rsqrt(var + GN_EPS)).reshape(B, L, D_RWKV) * ln_g + ln_b
    bonus = (jnp.sum(r * k * r_k, axis=-1, keepdims=True) * v).reshape(B, L, D_RWKV)
    gate = jax.nn.sigmoid(gl) @ g2
    return ((yn + bonus) * gate).astype(v.dtype)


def _hgrn2_features(p, lb):
    B, L, _ = p.shape
    q, f, i, g = _split(p, HGRN_SPLITS)
    hd = lambda t: t.reshape(B, L, HGRN_HEADS, -1)
    qh = hd(jax.nn.silu(q))
    fd = lb + (1 - lb) * jax.nn.sigmoid(f.reshape(B, L, 2, D_HGRN).astype(jnp.float32))
    dirs = [(qh, hd(1 - fd[:, :, d]), hd(i), hd(jnp.log(fd[:, :, d]))) for d in range(2)]
    return dirs, g


def _hgrn2_chunk_scan(q, k, v, log_f, s0):
    B, L, H, K = q.shape
    V = v.shape[-1]
    n = L // HGRN_CHUNK
    def chunks(t):
        return t.astype(jnp.float32).reshape(B, n, HGRN_CHUNK, H, t.shape[-1]).transpose(1, 0, 3, 2, 4)
    lower_tri = jnp.tril(jnp.ones((HGRN_CHUNK, HGRN_CHUNK), dtype=bool))[:, :, None]
    def step(S, inp):
        qc, kc, vc, gc = inp
        b = jnp.cumsum(gc, axis=2)
        rel = jnp.exp(jnp.where(lower_tri, b[:, :, :, None, :] - b[:, :, None, :, :], -jnp.inf))
        scores = jnp.einsum('bhtk,bhsk,bhtsk->bhts', qc, kc, rel)
        o = jnp.einsum('bhts,bhsv->bhtv', scores, vc) + jnp.einsum('bhtk,bhkv->bhtv', qc * jnp.exp(b), S)
        b_end = b[:, :, -1:, :]
        S = S * jnp.exp(b_end)[:, :, 0, :, None] + jnp.einsum('bhsk,bhsv->bhkv', kc * jnp.exp(b_end - b), vc)
        return S, o
    S, o = lax.scan(step, s0, (chunks(q), chunks(k), chunks(v), chunks(log_f)))
    return o.transpose(1, 0, 3, 2, 4).reshape(B, L, H, V), S


def _hgrn2_out(o, g, norm_g):
    B, L, H, V = o.shape
    on = o * lax.rsqrt(jnp.mean(o * o, axis=-1, keepdims=True) + NORM_EPS) * norm_g
    return (on.reshape(B, L, D_HGRN) * jax.nn.silu(g)).astype(g.dtype)


def _bidir_with_prefix(scan_fn, ctx_dirs, lat_dirs, s0):
    y_lat, y_ctx = [], []
    for d, rev in enumerate((False, True)):
        cin, lin = ctx_dirs[d], lat_dirs[d]
        if rev:
            cin = tuple(jnp.flip(t, 1) for t in cin)
            lin = tuple(jnp.flip(t, 1) for t in lin)
        yc, s_ctx = scan_fn(*cin, s0)
        yl, _ = scan_fn(*lin, s_ctx)
        if rev:
            yc, yl = jnp.flip(yc, 1), jnp.flip(yl, 1)
        y_lat.append(yl)
        y_ctx.append(yc)
    return y_lat[0] + y_lat[1], y_ctx[0] + y_ctx[1]


def _sq_relu_mlp(h, w1, w2):
    return jnp.square(jax.nn.relu(h @ w1)) @ w2


def setup_inputs(seed: int = 0) -> dict:
    key = jax.random.key(seed)
    ks = iter(jax.random.split(key, 40))
    nrm = lambda shape, scale: jax.random.normal(next(ks), shape, jnp.float32) * scale
    gain = lambda shape: 1.0 + nrm(shape, 0.02)
    unif = lambda shape, lo, hi: jax.random.uniform(next(ks), shape, jnp.float32, lo, hi)
    return {
        "x": nrm((BATCH, SEQ, D_MODEL), 1.0),
        "c": nrm((BATCH, D_MODEL), 1.0),
        "ctx": nrm((BATCH, CTX_LEN, D_MODEL), 1.0),
        "c_ctx": nrm((D_MODEL,), 1.0),
        "w_ada": nrm((DEPTH, D_MODEL, N_MOD * D_MODEL), 0.3 * D_MODEL ** -0.5),
        "b_ada": nrm((DEPTH, N_MOD * D_MODEL), 0.01),
        "g_mix_pre": gain((DEPTH, D_MODEL)),
        "g_mix_post": gain((DEPTH, D_MODEL)),
        "g_ffn_pre": gain((DEPTH, D_MODEL)),
        "g_ffn_post": gain((DEPTH, D_MODEL)),
        "w_in": nrm((DEPTH, D_MODEL, IN_COLS), D_MODEL ** -0.5),
        "mu_shift": unif((DEPTH, RWKV_COLS), 0.2, 0.8),
        "w0": unif((DEPTH, 2, D_RWKV), -6.0, 0.0),
        "w2": nrm((DEPTH, 2, D_DECAY_LORA, D_RWKV), 0.1 * D_DECAY_LORA ** -0.5),
        "a0": nrm((DEPTH, 2, D_RWKV), 0.1),
        "a2": nrm((DEPTH, 2, D_AAA_LORA, D_RWKV), 0.1 * D_AAA_LORA ** -0.5),
        "g2": nrm((DEPTH, D_GATE_LORA, D_RWKV), D_GATE_LORA ** -0.5),
        "k_k": 0.85 + nrm((DEPTH, D_RWKV), 0.02),
        "k_a": gain((DEPTH, D_RWKV)),
        "r_k": nrm((DEPTH, RWKV_HEADS, RWKV_HEAD), 0.1),
        "ln_x_g": gain((DEPTH, D_RWKV)),
        "ln_x_b": nrm((DEPTH, D_RWKV), 0.01),
        "hgrn_lb_logits": nrm((DEPTH + 1, 2, D_HGRN), 0.1),
        "hgrn_norm_g": gain((DEPTH, HGRN_VAL)),
        "w_out": nrm((DEPTH, D_MIX, D_MODEL), D_MIX ** -0.5),
        "w_ff1": nrm((DEPTH, D_MODEL, D_FF), D_MODEL ** -0.5),
        "w_ff2": nrm((DEPTH, D_FF, D_MODEL), D_FF ** -0.5),
    }


def reference(x, c, ctx, c_ctx, w_ada, b_ada, g_mix_pre, g_mix_post, g_ffn_pre, g_ffn_post, w_in,
              mu_shift, w0, w2, a0, a2, g2, k_k, k_a, r_k, ln_x_g, ln_x_b, hgrn_lb_logits,
              hgrn_norm_g, w_out, w_ff1, w_ff2):
    B = x.shape[0]
    rows = x.shape[1] // GRID_W
    lower_bounds = jnp.cumsum(jax.nn.softmax(hgrn_lb_logits.astype(jnp.float32), axis=0), axis=0)
    for l in range(DEPTH):
        last = l == DEPTH - 1
        mod_x = jax.nn.silu(c) @ w_ada[l] + b_ada[l]
        mod_c = jax.nn.silu(c_ctx) @ w_ada[l] + b_ada[l]
        sh1, sc1, gt1, sh2, sc2, gt2 = jnp.split(mod_x[:, None, :], N_MOD, axis=-1)
        csh1, csc1, cgt1, csh2, csc2, cgt2 = jnp.split(mod_c, N_MOD, axis=-1)

        px = _modulate(_rms_norm(x, g_mix_pre[l]), sh1, sc1) @ w_in[l]
        pc = _modulate(_rms_norm(ctx, g_mix_pre[l]), csh1, csc1) @ w_in[l]
        pax, phx = px[..., :RWKV_COLS], px[..., RWKV_COLS:]
        pac, phc = pc[..., :RWKV_COLS], pc[..., RWKV_COLS:]

        rw = (mu_shift[l], w0[l], w2[l], a0[l], a2[l], k_k[l], k_a[l])
        dirs_x, aux_x = _rwkv7_features(pax, _qshift_grid(pax, rows), *rw)
        dirs_c, aux_c = _rwkv7_features(pac, _qshift_seq(pac), *rw)
        s0_rwkv = jnp.zeros((B, RWKV_HEADS, RWKV_HEAD, RWKV_HEAD), jnp.float32)
        ya_x, ya_c = _bidir_with_prefix(_rwkv7_scan, dirs_c, dirs_x, s0_rwkv)

        hdirs_x, hg_x = _hgrn2_features(phx, lower_bounds[l])
        hdirs_c, hg_c = _hgrn2_features(phc, lower_bounds[l])
        s0_hgrn = jnp.zeros((B, HGRN_HEADS, HGRN_EXPAND, HGRN_VAL), jnp.float32)
        yh_x, yh_c = _bidir_with_prefix(_hgrn2_chunk_scan, hdirs_c, hdirs_x, s0_hgrn)

        rw_out = (r_k[l], ln_x_g[l], ln_x_b[l], g2[l])
        ux = jnp.concatenate([_rwkv7_out(ya_x, *aux_x, *rw_out),
                              _hgrn2_out(yh_x, hg_x, hgrn_norm_g[l])], axis=-1) @ w_out[l]
        x = x + gt1 * _rms_norm(ux, g_mix_post[l])

        hx = _modulate(_rms_norm(x, g_ffn_pre[l]), sh2, sc2)
        x = x + gt2 * _rms_norm(_sq_relu_mlp(hx, w_ff1[l], w_ff2[l]), g_ffn_post[l])

        if not last:
            uc = jnp.concatenate([_rwkv7_out(ya_c, *aux_c, *rw_out),
                                  _hgrn2_out(yh_c, hg_c, hgrn_norm_g[l])], axis=-1) @ w_out[l]
            ctx = ctx + cgt1 * _rms_norm(uc, g_mix_post[l])
            hc = _modulate(_rms_norm(ctx, g_ffn_pre[l]), csh2, csc2)
            ctx = ctx + cgt2 * _rms_norm(_sq_relu_mlp(hc, w_ff1[l], w_ff2[l]), g_ffn_post[l])
    return x
```

```python
import contextlib
import numpy as np
import concourse.bass as bass
import concourse.mybir as mybir
from concourse.bass_utils import run_bass_kernel_spmd

F32 = mybir.dt.float32
BF16 = mybir.dt.bfloat16
AF = mybir.ActivationFunctionType
ALU = mybir.AluOpType

D = 4096
SEQ = 8192
CTX = 256
NT = SEQ + CTX
NCORE = 8
TPC = SEQ // NCORE
DFF = 16384
CL = -0.6065306597126334
CW = [128] * 6 + [96] * 4 + [128] * 2 + [128] * 10
COFF = [int(v) for v in np.cumsum([0] + CW)]
NCOL = COFF[-1]
NA = 12
PV = {}
_o = 0
for _n, _w in [("gpre", 32), ("gpost", 32), ("gfpre", 32), ("gfpost", 32), ("mu", 12), ("w0", 4), ("a0", 4),
               ("kk", 2), ("ka", 2), ("rk", 2), ("lng", 2), ("lnb", 2), ("lbl", 8), ("hng", 1), ("cls", 4)]:
    PV[_n] = _o
    _o += _w
NPV = _o
CS = {}
_o = 0
for _n, _w in [("ident", 128), ("bo64", 128), ("ones", 128), ("mkr0", 320), ("mkr1", 320), ("mkh0", 64), ("mkh1", 64),
               ("reset", 512)]:
    CS[_n] = _o
    _o += _w
NCS = _o

DEBUG = {}


class Buf:
    __slots__ = ("name", "w", "r", "t")

    def __init__(self, name, t=None):
        self.name = name
        self.w = None
        self.r = []
        self.t = t


class Sched:
    ENGS = ("pe", "act", "dve", "pool", "sp")
    LIMIT = 10 ** 9
    CLEAR = False

    def __init__(self, nc, stack, n_dma_sems=10):
        self.nc = nc
        self.n_dma = n_dma_sems
        self.csem = {e: stack.enter_context(nc.semaphore("s_" + e)) for e in self.ENGS}
        self.dsem = [stack.enter_context(nc.semaphore("d_%d" % i)) for i in range(n_dma_sems)]
        self.total = 0
        self.epoch = 0
        self._reset()

    def _reset(self):
        self.ops = {e: [] for e in self.ENGS}
        self.cnt = {e: 0 for e in self.ENGS}
        self.seen = {e: {} for e in self.ENGS}
        self.dma_tot = [0] * self.n_dma

    def _waits_for(self, eng, deps):
        waits = {}
        for tg in deps:
            if tg is None or tg[3] != self.epoch:
                continue
            if tg[0] == "c":
                if tg[1] == "pe" and eng == "pe":
                    continue
                key = ("c", tg[1])
            else:
                key = ("d", tg[1])
            if self.seen[eng].get(key, 0) >= tg[2]:
                continue
            if waits.get(key, 0) < tg[2]:
                waits[key] = tg[2]
        for k, v in waits.items():
            self.seen[eng][k] = v
        return list(waits.items())

    def op(self, eng, fn, reads=(), writes=(), dma=None, inc=16, acc=False):
        if max(self.cnt.values()) >= self.LIMIT or max(self.dma_tot) >= self.LIMIT:
            self.flush()
        deps = []
        for b in reads:
            deps.append(b.w)
        if not acc:
            for b in writes:
                deps.append(b.w)
                deps.extend(b.r)
        waits = self._waits_for(eng, deps)
        if dma is not None:
            self.dma_tot[dma] += inc
            tag = ("d", dma, self.dma_tot[dma], self.epoch)
        else:
            self.cnt[eng] += 1
            tag = ("c", eng, self.cnt[eng], self.epoch)
        self.ops[eng].append((waits, fn, tag, inc))
        for b in reads:
            if b.r and b.r[0][3] != self.epoch:
                b.r = []
            b.r.append(tag)
        for b in writes:
            b.w = tag
            b.r = []
        self.total += 1
        return tag

    def _sem(self, key):
        return self.csem[key[1]] if key[0] == "c" else self.dsem[key[1]]

    def flush(self):
        nc = self.nc
        fin = self._waits_for("sp", [("d", i, v, self.epoch) for i, v in enumerate(self.dma_tot) if v])
        ops = self.ops
        with nc.Block() as block:
            def run(engname, e):
                for waits, fn, tag, inc in ops[engname]:
                    for key, val in waits:
                        e.wait_ge(self._sem(key), val)
                    ins = fn(e)
                    if tag[0] == "c":
                        ins.then_inc(self.csem[tag[1]], 1)
                    else:
                        ins.then_inc(self.dsem[tag[1]], inc)
                if engname == "sp":
                    for key, val in fin:
                        e.wait_ge(self._sem(key), val)

            @block.tensor
            def _(e):
                run("pe", e)

            @block.scalar
            def _(e):
                run("act", e)

            @block.vector
            def _(e):
                run("dve", e)

            @block.gpsimd
            def _(e):
                run("pool", e)

            @block.sync
            def _(e):
                run("sp", e)
        if self.CLEAR:
            with nc.Block() as block:
                @block.sync
                def _(e):
                    for sh in list(self.csem.values()) + self.dsem:
                        e.sem_clear(sh)
            self.epoch += 1
            self._reset()
        else:
            self.ops = {e: [] for e in self.ENGS}
            for e in self.ENGS:
                for e2 in self.ENGS:
                    self.seen[e][("c", e2)] = self.cnt[e2]
                for i, v in enumerate(self.dma_tot):
                    self.seen[e][("d", i)] = v


class Arena:
    N = [0]

    def __init__(self, nc):
        self.nc = nc
        self.st = contextlib.ExitStack()

    def sb(self, name, shape, dt):
        Arena.N[0] += 1
        return Buf(name, self.st.enter_context(self.nc.sbuf_tensor("a%d_%s" % (Arena.N[0], name), list(shape), dt)))

    def ps(self, name, shape, dt=F32):
        Arena.N[0] += 1
        return Buf(name, self.st.enter_context(self.nc.psum_tensor("a%d_%s" % (Arena.N[0], name), list(shape), dt)))

    def close(self):
        self.st.close()


class K:
    def __init__(self, dbg=None):
        self.dbg = dbg or {}
        nc = self.nc = bass.Bass("TRN2", target_bir_lowering=False)
        self.stack = contextlib.ExitStack()
        self.S = Sched(nc, self.stack)
        dkf = lambda n: "ExternalOutput" if n in self.dbg.get("dump", ()) else "Internal"
        di = lambda n, s, dt=F32: nc.dram_tensor(n, list(s), dt, kind="ExternalInput").ap()
        full = not self.dbg.get("no_stage2")
        gi = lambda n, rows, cols, ok=True: (
            di(n + "_s", [rows // NCORE, cols] if ok else [8, 8]),
            nc.dram_tensor(n + "_t", [rows // NCORE, cols] if ok else [8, 8], F32, kind="Internal").ap(),
            nc.dram_tensor(n + "_g", [rows, cols] if ok else [64, 8], F32, addr_space="Local", kind="Internal").ap())
        self.gsrc = {"xall": gi("xall", NT, D), "w_out": gi("w_out", D, D, full), "w_ff1": gi("w_ff1", D, DFF, full),
                     "w_ff2": gi("w_ff2", DFF, D, full)}
        self.xall = self.gsrc["xall"][2]
        self.xs = di("xs", [TPC, D])
        self.w_in_c = di("w_in_c", [D, NCOL])
        self.w_ada_c = di("w_ada_c", [D, 3072])
        self.b_ada_c = di("b_ada_c", [1, 3072])
        self.c2T = di("c2T", [128, 32, 2])
        self.pvd = di("pv", [128, NPV])
        self.cstd = di("cst", [128, NCS])
        self.w2_c = di("w2_c", [2, 96, 256])
        self.a2_c = di("a2_c", [2, 96, 256])
        self.g2_c = di("g2_c", [256, 256])
        self.w_out, self.w_ff1, self.w_ff2 = self.gsrc["w_out"][2], self.gsrc["w_ff1"][2], self.gsrc["w_ff2"][2]
        self.out = nc.dram_tensor("out", [TPC, D], F32, kind="ExternalOutput").ap()
        self.mod_in = nc.dram_tensor("mod_in", [2, 3072], F32, kind="Internal").ap()
        self.mod_all = nc.dram_tensor("mod_all", [16, 3072], F32, addr_space="Local", kind="Internal").ap()
        self.P_d = nc.dram_tensor("P_d", [22, 128, NT], F32, kind=dkf("P_d")).ap()
        self.xnT_d = nc.dram_tensor("xnT_d", [32, 128, NT], BF16, kind="Internal").ap()
        self.Y_d = nc.dram_tensor("Y_d", [2, 4, 128, SEQ], F32, kind=dkf("Y_d")).ap()
        self.u_d = nc.dram_tensor("u_d", [NCORE, 512, TPC], BF16, kind=dkf("u_d")).ap()
        self.u_all = nc.dram_tensor("u_all", [NCORE * NCORE, 512, TPC], BF16, addr_space="Local", kind="Internal").ap()
        self.x1_d = nc.dram_tensor("x1_d", [2, 32, 128, 512], F32, kind=dkf("x1_d")).ap()
        self.bufs = {n: Buf(n) for n in ["mod_in", "mod_all", "P_d", "xnT_d", "Y_d", "u_d", "u_all", "x1_d", "out"]}
        self.Pb = [Buf("P_d%d" % j) for j in range(22)]
        sb = lambda n, s, dt: Buf(n, self.stack.enter_context(nc.sbuf_tensor("sb_" + n, list(s), dt)))
        self.cst = sb("cst", [128, NCS], F32)
        self.pv = sb("pvs", [128, NPV], F32)
        self.modv = sb("modv", [128, 8, 32], F32)
        self.sm = sb("sm", [128, 128], F32)
        self.identb = sb("identb", [128, 128], BF16)
        self.cstb = sb("cstb", [128, 256], BF16)
        self.ndma = 0

    def dma(self, out, in_, reads, writes, q="sp", sem=None):
        if sem is None:
            sem = self.ndma % 6
            self.ndma += 1
        self.S.op(q, lambda e: e.dma_start(out=out, in_=in_), reads, writes, dma=sem)

    def mm(self, out, lhsT, rhs, start, stop, reads, writes):
        self.S.op("pe", lambda e: e.matmul(out, lhsT=lhsT, rhs=rhs, start=start, stop=stop), reads, writes, acc=not start)

    def tr(self, out, in_, ident, reads, writes, acc=False):
        self.S.op("pe", lambda e: e.transpose(out=out, in_=in_, identity=ident), reads, writes, acc=acc)

    def act(self, out, in_, func, reads, writes, scale=None, bias=None, accum=None):
        kw = {}
        if scale is not None:
            kw["scale"] = scale
        if bias is not None:
            kw["bias"] = bias
        if accum is not None:
            kw["accum_out"] = accum
        self.S.op("act", lambda e: e.activation(out=out, in_=in_, func=func, **kw), reads, writes)

    def tt(self, out, in0, in1, op, reads, writes, eng="dve"):
        self.S.op(eng, lambda e: e.tensor_tensor(out=out, in0=in0, in1=in1, op=op), reads, writes)

    def ts(self, out, in0, s1, s2, op0, op1, reads, writes, eng="dve"):
        if op1 is None:
            self.S.op(eng, lambda e: e.tensor_scalar(out=out, in0=in0, scalar1=s1, scalar2=None, op0=op0), reads, writes)
        else:
            self.S.op(eng, lambda e: e.tensor_scalar(out=out, in0=in0, scalar1=s1, scalar2=s2, op0=op0, op1=op1), reads, writes)

    def stt(self, out, in0, scalar, in1, op0, op1, reads, writes):
        self.S.op("dve", lambda e: e.scalar_tensor_tensor(out=out, in0=in0, scalar=scalar, in1=in1, op0=op0, op1=op1), reads, writes)

    def cp(self, out, in_, reads, writes, eng="dve"):
        self.S.op(eng, lambda e: e.tensor_copy(out=out, in_=in_), reads, writes)

    def memset(self, ap, val, writes, eng="pool"):
        self.S.op(eng, lambda e: e.memset(ap, val), (), writes)

    def recip(self, out, in_, reads, writes):
        self.S.op("dve", lambda e: e.reciprocal(out=out, in_=in_), reads, writes)

    def statmm(self, out_ps, psbuf, wname, src_ap, srcbuf, hi, lo, n, start=True, stop=True):
        w = self.cstb.t[:, 0:128] if wname == "bo64" else self.cstb.t[:, 128:256]
        self.act(hi.t[:, 0:n], src_ap, AF.Copy, [srcbuf], [hi])
        self.tt(lo.t[:, 0:n], src_ap, hi.t[:, 0:n], ALU.subtract, [srcbuf, hi], [lo])
        self.mm(out_ps, w, hi.t[:, 0:n], start, False, [self.cstb, hi], [psbuf])
        self.mm(out_ps, w, lo.t[:, 0:n], False, stop, [self.cstb, lo], [psbuf])

    def C(self, name, w=None, p0=0, p1=128):
        o = CS[name]
        if w is None:
            w = {"ident": 128, "bo64": 128, "ones": 128, "mkr0": 320, "mkr1": 320, "mkh0": 64, "mkh1": 64, "reset": 512}[name]
        return self.cst.t[p0:p1, o:o + w]

    def V(self, name, i=0, p0=0, p1=128):
        o = PV[name] + i
        return self.pv.t[p0:p1, o:o + 1]

    def MV(self, v, t, p0=0, p1=128):
        return self.modv.t[p0:p1, v, t:t + 1]

    def SM(self, c, p0=0, p1=128):
        return self.sm.t[p0:p1, c:c + 1]

    def phase0(self):
        nc, S = self.nc, self.S
        A = Arena(nc)
        cst, pv, sm = self.cst, self.pv, self.sm
        self.dma(cst.t[:], self.cstd, [], [cst])
        self.dma(pv.t[:], self.pvd, [], [pv])
        self.cp(self.identb.t[:], self.C("ident"), [cst], [self.identb])
        self.cp(self.cstb.t[:, 0:128], self.C("bo64"), [cst], [self.cstb])
        self.cp(self.cstb.t[:, 128:256], self.C("ones"), [cst], [self.cstb])
        mu = pv.t[:, PV["mu"]:PV["mu"] + 12]
        self.ts(sm.t[:, 0:12], mu, -1.0, 1.0, ALU.mult, ALU.add, [pv], [sm])
        for j in range(4):
            self.ts(sm.t[:, 12 + 12 * j:24 + 12 * j], mu, self.V("cls", j), None, ALU.mult, None, [pv], [sm])
        self.tt(sm.t[:, 60:72], sm.t[:, 12:24], sm.t[:, 36:48], ALU.add, [sm], [sm])
        self.tt(sm.t[:, 72:84], sm.t[:, 24:36], sm.t[:, 48:60], ALU.add, [sm], [sm])
        self.ts(sm.t[:, 84:86], pv.t[:, PV["ka"]:PV["ka"] + 2], -1.0, 1.0, ALU.mult, ALU.add, [pv], [sm])
        lo = PV["lbl"]
        self.tt(sm.t[:, 86:90], pv.t[:, lo:lo + 4], pv.t[:, lo + 4:lo + 8], ALU.subtract, [pv], [sm])
        self.act(sm.t[:, 86:90], sm.t[:, 86:90], AF.Sigmoid, [sm], [sm])
        self.ts(sm.t[:, 90:94], sm.t[:, 86:90], -1.0, 1.0, ALU.mult, ALU.add, [sm], [sm])
        self.memset(sm.t[:, 94:95], 1e-6, [sm])
        self.memset(sm.t[:, 95:96], 1e-12, [sm])
        self.memset(sm.t[:, 96:97], 64e-5, [sm])
        silT = A.sb("silT", [128, 32, 2], F32)
        self.dma(silT.t[:], self.c2T, [], [silT])
        self.act(silT.t[:], silT.t[:], AF.Silu, [silT], [silT])
        wa = [A.sb("wa", [128, 8, 512], F32) for _ in range(2)]
        mp = [A.ps("mp", [128, 512]) for _ in range(2)]
        modrow = A.sb("modrow", [2, 3072], F32)
        brow = A.sb("brow", [2, 3072], F32)
        self.dma(brow.t[:], self.b_ada_c.partition_broadcast(2), [], [brow])
        n = 0
        for cc in range(6):
            for kg in range(4):
                w = wa[n % 2]
                n += 1
                self.dma(w.t[:], self.w_ada_c[kg * 1024:(kg + 1) * 1024, cc * 512:(cc + 1) * 512].rearrange("(k p) n -> p k n", p=128), [], [w])
                for k in range(8):
                    kt = kg * 8 + k
                    self.mm(mp[cc % 2].t[0:2, :], silT.t[:, kt, :], w.t[:, k, :], kt == 0, kt == 31, [silT, w], [mp[cc % 2]])
            self.tt(modrow.t[:, cc * 512:(cc + 1) * 512], mp[cc % 2].t[0:2, :], brow.t[:, cc * 512:(cc + 1) * 512], ALU.add,
                    [mp[cc % 2], brow], [modrow])
        B = self.bufs
        self.dma(self.mod_in, modrow.t[:], [modrow], [B["mod_in"]])
        S.op("pool", lambda e: e.collective_compute("AllGather", ALU.bypass, replica_groups=[list(range(NCORE))],
                                                    ins=[self.mod_in], outs=[self.mod_all]),
             [B["mod_in"]], [B["mod_all"]], dma=9, inc=1)
        modg = A.sb("modg", [16, 3072], F32)
        self.dma(modg.t[:], self.mod_all, [B["mod_all"]], [modg])
        pst = A.ps("pst", [128, 384])
        for tt in range(24):
            self.tr(pst.t[:, tt * 16:(tt + 1) * 16], modg.t[0:16, tt * 128:(tt + 1) * 128], self.C("ident", 16, 0, 16),
                    [modg, cst], [pst], acc=(tt != 0))
        modT = A.sb("modT", [128, 384], F32)
        self.cp(modT.t[:], pst.t[:], [pst], [modT])
        raw = A.sb("raw", [128, 8, 32], F32)
        m4 = modT.t[:].rearrange("p (t r w) -> p t r w", r=8, w=2)
        for i, (v, row) in enumerate([(0, 0), (1, 0), (2, 0), (3, 0), (4, 0), (5, 0), (0, 1), (1, 1)]):
            T = v * 32
            while T < v * 32 + 32:
                R = T // 24
                Te = min((R + 1) * 24, v * 32 + 32)
                self.cp(raw.t[:, i, T - v * 32:Te - v * 32], m4[:, T - R * 24:Te - R * 24, R, row], [modT], [raw])
                T = Te
        mv = self.modv
        g = lambda n: pv.t[:, PV[n]:PV[n] + 32]
        self.stt(mv.t[:, 0, :], raw.t[:, 1, :], 1.0, g("gpre"), ALU.add, ALU.mult, [raw, pv], [mv])
        self.cp(mv.t[:, 1, :], raw.t[:, 0, :], [raw], [mv])
        self.stt(mv.t[:, 2, :], raw.t[:, 7, :], 1.0, g("gpre"), ALU.add, ALU.mult, [raw, pv], [mv])
        self.cp(mv.t[:, 3, :], raw.t[:, 6, :], [raw], [mv])
        self.tt(mv.t[:, 4, :], raw.t[:, 2, :], g("gpost"), ALU.mult, [raw, pv], [mv])
        self.stt(mv.t[:, 5, :], raw.t[:, 4, :], 1.0, g("gfpre"), ALU.add, ALU.mult, [raw, pv], [mv])
        self.cp(mv.t[:, 6, :], raw.t[:, 3, :], [raw], [mv])
        self.tt(mv.t[:, 7, :], raw.t[:, 5, :], g("gfpost"), ALU.mult, [raw, pv], [mv])
        S.flush()
        A.close()

    def phase1(self):
        nc, S = self.nc, self.S
        TT = 256
        ntile = self.dbg.get("p1_tiles", NT // TT)
        for pas in range(2):
            A = Arena(nc)
            cols = list(range(0, NA)) if pas == 0 else list(range(NA, 22))
            c0, c1 = COFF[cols[0]], COFF[cols[-1] + 1]
            W = A.sb("W", [128, 32, c1 - c0], BF16)
            for kg in range(4):
                self.dma(W.t[:, kg * 8:(kg + 1) * 8, :],
                         self.w_in_c[kg * 1024:(kg + 1) * 1024, c0:c1].rearrange("(k p) n -> p k n", p=128), [], [W], q="pool", sem=6)
            xb = [A.sb("xb", [128, D], F32) for _ in range(2)]
            xr = A.sb("xr", [128, D], F32)
            junk = A.sb("junk", [128, D], BF16)
            st = [A.sb("st", [128, 4], F32) for _ in range(2)]
            xnT = [A.sb("xnT", [128, 32, TT], BF16) for _ in range(2)]
            stg = [A.sb("stg", [128, TT], F32) for _ in range(4)]
            ptr = [A.ps("ptr", [128, 512]) for _ in range(3)]
            pmm = [A.ps("pmm", [128, 512]) for _ in range(4)]
            nb = 0
            ne = 0
            for ti in range(ntile):
                t0 = ti * TT
                xn = xnT[ti % 2]
                if pas == 0:
                    for blk in range(TT // 128):
                        x = xb[nb % 2]
                        s = st[nb % 2]
                        nb += 1
                        tb = t0 + blk * 128
                        self.dma(x.t[:], self.xall[tb:tb + 128, :], [], [x])
                        self.act(junk.t[:], x.t[:], AF.Square, [x], [junk, s], accum=s.t[:, 0:1])
                        self.act(s.t[:, 1:2], s.t[:, 0:1], AF.Sqrt, [s, self.sm], [s], scale=1.0 / D, bias=self.SM(94))
                        self.recip(s.t[:, 2:3], s.t[:, 1:2], [s], [s])
                        self.act(xr.t[:], x.t[:], AF.Copy, [x, s], [xr], scale=s.t[:, 2:3])
                        vg, vs = (2, 3) if tb < CTX else (0, 1)
                        for kq in range(8):
                            p = ptr[kq % 3]
                            for k in range(4):
                                kt = kq * 4 + k
                                self.tr(p.t[:, k * 128:(k + 1) * 128], xr.t[:, kt * 128:(kt + 1) * 128], self.C("ident"),
                                        [xr, self.cst], [p], acc=(k != 0))
                            for k in range(4):
                                kt = kq * 4 + k
                                o = xn.t[:, kt, blk * 128:(blk + 1) * 128]
                                if ne % 2 == 0:
                                    self.act(o, p.t[:, k * 128:(k + 1) * 128], AF.Identity, [p, self.modv], [xn],
                                             scale=self.MV(vg, kt), bias=self.MV(vs, kt))
                                else:
                                    self.ts(o, p.t[:, k * 128:(k + 1) * 128], self.MV(vg, kt), self.MV(vs, kt), ALU.mult, ALU.add,
                                            [p, self.modv], [xn])
                                ne += 1
                    self.dma(self.xnT_d[:, :, t0:t0 + TT].rearrange("k p t -> p k t"), xn.t[:], [xn], [self.bufs["xnT_d"]])
                else:
                    self.dma(xn.t[:], self.xnT_d[:, :, t0:t0 + TT].rearrange("k p t -> p k t"), [self.bufs["xnT_d"]], [xn])
                for ji, j in enumerate(cols):
                    w = CW[j]
                    o0 = COFF[j] - c0
                    p = pmm[ji % 4]
                    for kt in range(32):
                        self.mm(p.t[0:w, 0:TT], W.t[:, kt, o0:o0 + w], xn.t[:, kt, :], kt == 0, kt == 31, [W, xn], [p])
                    sg = stg[ji % 4]
                    if ji % 2 == 0:
                        self.act(sg.t[0:w, :], p.t[0:w, 0:TT], AF.Copy, [p], [sg])
                    else:
                        self.cp(sg.t[0:w, :], p.t[0:w, 0:TT], [p], [sg])
                    self.dma(self.P_d[j, 0:w, t0:t0 + TT], sg.t[0:w, :], [sg], [self.Pb[j]])
            S.flush()
            A.close()

    def phase_shift(self):
        nc, S = self.nc, self.S
        A = Arena(nc)
        pb = [A.sb("pb", [128, NT], F32) for _ in range(2)]
        mb = [A.sb("mb", [128, NT], F32) for _ in range(2)]
        for j in range(NA):
            w = CW[j]
            p, m = pb[j % 2], mb[j % 2]
            self.dma(p.t[0:w, :], self.P_d[j, 0:w, :], [self.Pb[j]], [p])
            self.act(m.t[0:w, :], p.t[0:w, :], AF.Copy, [p, self.sm], [m], scale=self.SM(j, 0, w))
            cf = lambda q: self.SM(12 + 12 * q + j, 0, w)
            px = p.t[0:w, CTX:NT]
            mx = m.t[0:w, CTX:NT]
            p3 = px.rearrange("p (r c) -> p r c", c=64)
            m3 = mx.rearrange("p (r c) -> p r c", c=64)
            rw = [p, m, self.sm]
            self.stt(m3[:, :, 1:64], p3[:, :, 0:63], cf(0), m3[:, :, 1:64], ALU.mult, ALU.add, rw, [m])
            self.stt(m3[:, :, 0:63], p3[:, :, 1:64], cf(1), m3[:, :, 0:63], ALU.mult, ALU.add, rw, [m])
            self.stt(mx[:, 64:SEQ], px[:, 0:SEQ - 64], cf(2), mx[:, 64:SEQ], ALU.mult, ALU.add, rw, [m])
            self.stt(mx[:, 0:SEQ - 64], px[:, 64:SEQ], cf(3), mx[:, 0:SEQ - 64], ALU.mult, ALU.add, rw, [m])
            pc = p.t[0:w, 0:CTX]
            mc = m.t[0:w, 0:CTX]
            self.stt(mc[:, 1:CTX], pc[:, 0:CTX - 1], cf(4), mc[:, 1:CTX], ALU.mult, ALU.add, rw, [m])
            self.stt(mc[:, 0:CTX - 1], pc[:, 1:CTX], cf(5), mc[:, 0:CTX - 1], ALU.mult, ALU.add, rw, [m])
            self.dma(self.P_d[j, 0:w, :], m.t[0:w, :], [m], [self.Pb[j]])
        S.flush()
        A.close()


def _consts():
    c = np.zeros((128, NCS), np.float32)
    p = np.arange(128)
    c[:, CS["ident"]:CS["ident"] + 128] = np.eye(128, dtype=np.float32)
    c[:, CS["bo64"]:CS["bo64"] + 128] = (p[:, None] // 64 == p[None, :] // 64)
    c[:, CS["ones"]:CS["ones"] + 128] = 1.0
    s = (p % 64)[:, None]
    t = np.arange(64)[None, :]
    lt, le, gt, ge = (s < t), (s <= t), (s > t), (s >= t)
    c[:, CS["mkr0"]:CS["mkr0"] + 320] = np.concatenate([lt, le, lt, le, gt], 1)
    c[:, CS["mkr1"]:CS["mkr1"] + 320] = np.concatenate([gt, ge, gt, ge, lt], 1)
    c[:, CS["mkh0"]:CS["mkh0"] + 64] = le
    c[:, CS["mkh1"]:CS["mkh1"] + 64] = ge
    c[:, CS["reset"]:CS["reset"] + 512] = (np.arange(512) % 64 != 0)[None, :]
    return c


def _prep(inp):
    f = lambda k: np.asarray(inp[k], np.float32)
    x, ctx = f("x")[0], f("ctx")[0]
    xall = np.ascontiguousarray(np.concatenate([ctx, x], 0))
    c2 = np.stack([f("c")[0], f("c_ctx")], 0)
    c2T = np.ascontiguousarray(c2.reshape(2, 32, 128).transpose(2, 1, 0))
    w_in, w_ada, b_ada = f("w_in")[0], f("w_ada")[0], f("b_ada")[0]
    w_out, w_ff1, w_ff2 = f("w_out")[0], f("w_ff1")[0], f("w_ff2")[0]
    colT = lambda v: v.reshape(32, 128).T
    cst = _consts()
    p = np.arange(128)
    maps = []
    for c in range(NCORE):
        b = 256 * c
        H = 6784
        cols = np.concatenate([np.arange(b, b + 256), 2048 + np.arange(b, b + 256), 4096 + np.arange(b, b + 256),
                               np.arange(6144, 6528), np.arange(6528, 6784),
                               H + np.arange(b, b + 256), H + 2048 + np.arange(b, b + 256), H + 4096 + np.arange(b, b + 256),
                               H + 6144 + np.arange(b, b + 256), H + 8192 + np.arange(b, b + 256)])
        assert cols.shape[0] == NCOL
        pv = np.zeros((128, NPV), np.float32)
        pv[:, PV["gpre"]:PV["gpre"] + 32] = colT(f("g_mix_pre")[0])
        pv[:, PV["gpost"]:PV["gpost"] + 32] = colT(f("g_mix_post")[0])
        pv[:, PV["gfpre"]:PV["gfpre"] + 32] = colT(f("g_ffn_pre")[0])
        pv[:, PV["gfpost"]:PV["gfpost"] + 32] = colT(f("g_ffn_post")[0])
        mu = f("mu_shift")[0]
        for j in range(NA):
            pv[:CW[j], PV["mu"] + j] = mu[cols[COFF[j]:COFF[j + 1]]]
        for d in range(2):
            for hp in range(2):
                sl = slice(b + hp * 128, b + hp * 128 + 128)
                pv[:, PV["w0"] + d * 2 + hp] = f("w0")[0, d, sl]
                pv[:, PV["a0"] + d * 2 + hp] = f("a0")[0, d, sl]
        for hp in range(2):
            sl = slice(b + hp * 128, b + hp * 128 + 128)
            pv[:, PV["kk"] + hp] = f("k_k")[0, sl]
            pv[:, PV["ka"] + hp] = f("k_a")[0, sl]
            pv[:, PV["rk"] + hp] = f("r_k")[0].reshape(-1)[sl]
            pv[:, PV["lng"] + hp] = f("ln_x_g")[0, sl]
            pv[:, PV["lnb"] + hp] = f("ln_x_b")[0, sl]
        lbl = f("hgrn_lb_logits")
        for sl_ in range(2):
            for d in range(2):
                for h in range(2):
                    pv[:, PV["lbl"] + sl_ * 4 + d * 2 + h] = lbl[sl_, d, b + 128 * h:b + 128 * h + 128]
        pv[:, PV["hng"]] = f("hgrn_norm_g")[0]
        for j in range(4):
            pv[:, PV["cls"] + j] = (p % 4 == j)
        maps.append({
            "xall_s": np.ascontiguousarray(xall[(NT // NCORE) * c:(NT // NCORE) * (c + 1)]),
            "xs": np.ascontiguousarray(x[TPC * c:TPC * (c + 1)]),
            "w_in_c": np.ascontiguousarray(w_in[:, cols]),
            "w_ada_c": np.ascontiguousarray(w_ada[:, 3072 * c:3072 * (c + 1)]),
            "b_ada_c": np.ascontiguousarray(b_ada[None, 3072 * c:3072 * (c + 1)]),
            "c2T": c2T, "pv": pv, "cst": cst,
            "w2_c": np.ascontiguousarray(f("w2")[0][:, :, b:b + 256]),
            "a2_c": np.ascontiguousarray(f("a2")[0][:, :, b:b + 256]),
            "g2_c": np.ascontiguousarray(f("g2")[0][:, b:b + 256]),
            "w_out_s": np.ascontiguousarray(w_out[512 * c:512 * (c + 1)]),
            "w_ff1_s": np.ascontiguousarray(w_ff1[512 * c:512 * (c + 1)]),
            "w_ff2_s": np.ascontiguousarray(w_ff2[2048 * c:2048 * (c + 1)]),
        })
    return maps


def _segs():
    return [(0, CTX)] + [(CTX + 512 * i, 512) for i in range(SEQ // 512)]


class K2(K):
    def phase_rwkv(self):
        nc, S = self.nc, self.S
        A = Arena(nc)
        f32t = lambda n, s=(128, 512): A.sb(n, s, F32)
        w2b = A.sb("w2b", [96, 2, 256], BF16)
        a2b = A.sb("a2b", [96, 2, 256], BF16)
        self.dma(w2b.t[:], self.w2_c.rearrange("d r c -> r d c"), [], [w2b], q="pool", sem=6)
        self.dma(a2b.t[:], self.a2_c.rearrange("d r c -> r d c"), [], [a2b], q="pool", sem=6)
        id2 = A.sb("id2", [128, 64], BF16)
        self.tt(id2.t[:], self.C("ident", 64), self.cst.t[:, CS["ident"] + 64:CS["ident"] + 128], ALU.add, [self.cst], [id2])
        L = []
        for hp in range(2):
            l = {}
            for n in ["mr", "mk", "mv", "mwl", "mal", "kk", "T1", "SG", "AD", "P", "X1", "X2", "E1"]:
                l[n] = f32t(n)
            l["th"] = A.sb("th", [128, 512], BF16)
            l["malb"] = A.sb("malb", [128, 512], BF16)
            l["shi"] = A.sb("shi", [128, 512], BF16)
            l["slo"] = A.sb("slo", [128, 512], BF16)
            for n in ["BT", "KT", "Bh", "Kh", "Vb"]:
                l[n] = A.sb(n, [128, 512], BF16)
            l["ART"] = [A.sb("ART", [128, 8, 2, 64], BF16) for _ in range(2)]
            l["LM"] = [A.sb("LM", [128, 8, 320], BF16) for _ in range(2)]
            l["TTs"] = [A.sb("TTs", [128, 8, 64], BF16) for _ in range(2)]
            l["tok"] = [A.sb("tok", [128, 8, 192], BF16) for _ in range(2)]
            l["GC"] = [A.sb("GC", [128, 8], F32) for _ in range(2)]
            l["Y"] = [f32t("Y") for _ in range(2)]
            l["TTw"] = [A.sb("TTw", [128, 64], BF16) for _ in range(3)]
            l["XZ"] = [A.sb("XZ", [128, 128], BF16) for _ in range(3)]
            l["H"] = A.sb("H", [128, 64], F32)
            l["Hb"] = A.sb("Hb", [128, 64], BF16)
            l["RHSb"] = A.sb("RHSb", [128, 64], BF16)
            l["Ub"] = A.sb("Ub", [128, 64], BF16)
            L.append(l)
        pf = [A.ps("pf", [128, 512]) for _ in range(1)]
        pa_t = [A.ps("pa", [128, 512]) for _ in range(2)]
        py = [A.ps("py", [128, 512]) for _ in range(2)]
        pa = [(t, t, t, t) for t in pa_t]
        ptp = A.ps("pt", [128, 1024], BF16)
        pc = [A.ps("pc", [128, 512]) for _ in range(2)]
        pcb = [[pc[i], pc[i], pc[i], pc[i]] for i in range(2)]
        segs = _segs()
        if self.dbg.get("nseg"):
            segs = segs[:self.dbg["nseg"]]
        npf = 0
        npa = 0
        nw = 0
        for d in range(2):
            mk = self.C("mkr%d" % d)
            for hp in range(2):
                self.memset(L[hp]["H"].t[:], 0.0, [L[hp]["H"]])
                self.memset(L[hp]["Hb"].t[:], 0.0, [L[hp]["Hb"]])
            order = segs if d == 0 else [segs[0]] + segs[:0:-1]
            for si, (t0, Ln) in enumerate(order):
                nch = Ln // 64
                par = si % 2
                for hp in range(2):
                    l = L[hp]
                    v = lambda n: l[n].t[:, 0:Ln]
                    v3 = lambda n: l[n].t[:, 0:Ln].rearrange("p (c t) -> p c t", t=64)
                    for n, j in [("mr", 0 + hp), ("mk", 2 + hp), ("mv", 4 + hp)]:
                        self.dma(v(n), self.P_d[j, :, t0:t0 + Ln], [self.Pb[j]], [l[n]])
                    self.dma(l["mwl"].t[0:96, 0:Ln], self.P_d[6 + d, 0:96, t0:t0 + Ln], [self.Pb[6 + d]], [l["mwl"]])
                    self.dma(l["mal"].t[0:96, 0:Ln], self.P_d[8 + d, 0:96, t0:t0 + Ln], [self.Pb[8 + d]], [l["mal"]])
                    self.act(l["th"].t[0:96, 0:Ln], l["mwl"].t[0:96, 0:Ln], AF.Tanh, [l["mwl"]], [l["th"]])
                    p = pf[0]; npf += 1
                    self.mm(p.t[:, 0:Ln], w2b.t[:, d, hp * 128:(hp + 1) * 128], l["th"].t[0:96, 0:Ln], True, True, [w2b, l["th"]], [p])
                    self.act(v("SG"), p.t[:, 0:Ln], AF.Sigmoid, [p, self.pv], [l["SG"]], bias=self.V("w0", d * 2 + hp))
                    self.cp(l["malb"].t[0:96, 0:Ln], l["mal"].t[0:96, 0:Ln], [l["mal"]], [l["malb"]], eng="pool")
                    p = pf[0]; npf += 1
                    self.mm(p.t[:, 0:Ln], a2b.t[:, d, hp * 128:(hp + 1) * 128], l["malb"].t[0:96, 0:Ln], True, True, [a2b, l["malb"]], [p])
                    self.act(v("AD"), p.t[:, 0:Ln], AF.Sigmoid, [p, self.pv], [l["AD"]], bias=self.V("a0", d * 2 + hp))
                    self.ts(v("kk"), v("mk"), self.V("kk", hp), None, ALU.mult, None, [l["mk"], self.pv], [l["kk"]])
                    self.tt(v("T1"), v("kk"), v("kk"), ALU.mult, [l["kk"]], [l["T1"]], eng="pool")
                    p = pf[0]; npf += 1
                    self.statmm(p.t[:, 0:Ln], p, "bo64", v("T1"), l["T1"], l["shi"], l["slo"], Ln)
                    self.act(v("T1"), p.t[:, 0:Ln], AF.Sqrt, [p, self.sm], [l["T1"]], bias=self.SM(95))
                    self.recip(v("T1"), v("T1"), [l["T1"]], [l["T1"]])
                    self.tt(v("kk"), v("kk"), v("T1"), ALU.mult, [l["kk"], l["T1"]], [l["kk"]])
                    self.ts(v("T1"), v("AD"), self.V("ka", hp), self.SM(84 + hp), ALU.mult, ALU.add, [l["AD"], self.pv, self.sm], [l["T1"]])
                    self.tt(v("T1"), v("T1"), v("mk"), ALU.mult, [l["T1"], l["mk"]], [l["T1"]])
                    self.tt(v("AD"), v("AD"), v("kk"), ALU.mult, [l["AD"], l["kk"]], [l["AD"]], eng="pool")
                    S.op("dve", lambda e, o=v("P"), m=self.cst.t[:, CS["reset"]:CS["reset"] + Ln], s=v("SG"):
                         e.tensor_tensor_scan(out=o, data0=m, data1=s, initial=0.0, op0=ALU.mult, op1=ALU.add),
                         [self.cst, l["SG"]], [l["P"]])
                    GC = l["GC"][par]
                    self.act(GC.t[:, 0:nch], v3("P")[:, :, 63], AF.Exp, [l["P"]], [GC], scale=CL)
                    self.tt(v("X1"), v("P"), v("SG"), ALU.subtract, [l["P"], l["SG"]], [l["X1"]], eng="pool")
                    self.tt(v3("X2"), v3("P")[:, :, 63:64].to_broadcast([128, nch, 64]), v3("P"), ALU.subtract, [l["P"]], [l["X2"]])
                    if d == 0:
                        qe, qt, qi = "X1", "X2", "P"
                    else:
                        qe, qt, qi = "X2", "X1", "SG"
                        self.tt(v("SG"), v("X2"), v("SG"), ALU.add, [l["X2"], l["SG"]], [l["SG"]])
                    self.act(v("E1"), v(qi), AF.Exp, [l[qi]], [l["E1"]], scale=CL)
                    self.act(v(qi), v(qi), AF.Exp, [l[qi]], [l[qi]], scale=-CL)
                    self.act(v(qe), v(qe), AF.Exp, [l[qe]], [l[qe]], scale=CL)
                    self.act(v(qt), v(qt), AF.Exp, [l[qt]], [l[qt]], scale=CL)
                    ART = l["ART"][par]
                    self.stt(ART.t[:, 0:nch, 0, :], v3("kk"), -1.0, v3(qe), ALU.mult, ALU.mult, [l["kk"], l[qe]], [ART])
                    self.tt(ART.t[:, 0:nch, 1, :], v3("mr"), v3("E1"), ALU.mult, [l["mr"], l["E1"]], [ART], eng="pool")
                    self.tt(v("BT"), v("AD"), v(qi), ALU.mult, [l["AD"], l[qi]], [l["BT"]])
                    self.tt(v("KT"), v("T1"), v(qi), ALU.mult, [l["T1"], l[qi]], [l["KT"]], eng="pool")
                    self.tt(v("Bh"), v("AD"), v(qt), ALU.mult, [l["AD"], l[qt]], [l["Bh"]])
                    self.tt(v("Kh"), v("T1"), v(qt), ALU.mult, [l["T1"], l[qt]], [l["Kh"]], eng="pool")
                    self.cp(v("Vb"), v("mv"), [l["mv"]], [l["Vb"]], eng="pool")
                    LM, TTs, tok = l["LM"][par], l["TTs"][par], l["tok"][par]
                    rstop = self.dbg.get("rstop", 9)
                    for c in range(nch if rstop >= 2 else 0):
                        cs = slice(c * 64, (c + 1) * 64)
                        bank, bLM, bXZ, bTT = pa[npa % 2]; npa += 1
                        for h in range(2):
                            hs = slice(64 * h, 64 * h + 64)
                            art = ART.t[hs, c, :, :].rearrange("p a t -> p (a t)")
                            self.mm(bank.t[hs, 0:128], l["BT"].t[hs, cs], art, True, True, [l["BT"], ART], [bLM])
                            self.mm(bank.t[hs, 128:256], l["KT"].t[hs, cs], art, True, True, [l["KT"], ART], [bLM])
                            self.mm(bank.t[hs, 256:320], ART.t[hs, c, 0, :], l["BT"].t[hs, cs], True, True, [l["BT"], ART], [bLM])
                        self.tt(LM.t[:, c, :], bank.t[:, 0:320], mk, ALU.mult, [bLM, self.cst], [LM])
                        for h in range(2):
                            hs = slice(64 * h, 64 * h + 64)
                            ib = self.identb.t[hs, 64 * h:64 * h + 64]
                            self.tr(ptp.t[hs, 0:64], l["Bh"].t[hs, cs], ib, [l["Bh"], self.identb], [ptp])
                            self.tr(ptp.t[hs, 64:128], l["Kh"].t[hs, cs], ib, [l["Kh"], self.identb], [ptp], acc=True)
                            self.tr(ptp.t[hs, 128:192], l["Vb"].t[hs, cs], ib, [l["Vb"], self.identb], [ptp], acc=True)
                        self.act(tok.t[:, c, :], ptp.t[:, 0:192], AF.Copy, [ptp], [tok])
                        TT = l["TTw"][nw % 3]; nw += 1
                        self.tt(TT.t[:], LM.t[:, c, 0:64], id2.t[:], ALU.add, [LM, id2], [TT], eng="pool")
                        Xap = lambda hs_: LM.t[hs_, c, 256:320]
                        Zap = lambda hs_: LM.t[hs_, c, 0:64]
                        Xb = Zb = LM
                        for lev in range(5 if rstop >= 3 else 0):
                            for h in range(2):
                                hs = slice(64 * h, 64 * h + 64)
                                self.mm(bank.t[hs, 320:384], Zap(hs), Xap(hs), True, True, [Xb], [bXZ])
                                self.mm(bank.t[hs, 384:448], Xap(hs), Zap(hs), True, True, [Xb], [bXZ])
                            XZ = l["XZ"][nw % 3]; nw += 1
                            self.act(XZ.t[:], bank.t[:, 320:448], AF.Copy, [bXZ], [XZ])
                            Xap = lambda hs_, XZ=XZ: XZ.t[hs_, 0:64]
                            Zap = lambda hs_, XZ=XZ: XZ.t[hs_, 64:128]
                            Xb = XZ
                            for h in range(2):
                                hs = slice(64 * h, 64 * h + 64)
                                self.mm(bank.t[hs, 448:512], Xap(hs), TT.t[hs, :], True, True, [XZ, TT], [bTT])
                            if lev < 4:
                                TTn = l["TTw"][nw % 3]; nw += 1
                                self.tt(TTn.t[:], bank.t[:, 448:512], TT.t[:], ALU.add, [bTT, TT], [TTn])
                                TT = TTn
                            else:
                                self.tt(TTs.t[:, c, :], bank.t[:, 448:512], TT.t[:], ALU.add, [bTT, TT], [TTs])
                corder = list(range(nch)) if d == 0 else list(range(nch - 1, -1, -1))
                if self.dbg.get("rstop", 9) < 4:
                    corder = []
                for c in corder:
                    cs = slice(c * 64, (c + 1) * 64)
                    for hp in range(2):
                        l = L[hp]
                        ART, LM, TTs, tok = l["ART"][par], l["LM"][par], l["TTs"][par], l["tok"][par]
                        bR, bU, bY, bH = pcb[hp]
                        for h in range(2):
                            hs = slice(64 * h, 64 * h + 64)
                            self.mm(pc[hp].t[hs, 0:64], ART.t[hs, c, 0, :], l["Hb"].t[hs, :], True, False, [ART, l["Hb"]], [bR])
                            self.mm(pc[hp].t[hs, 0:64], LM.t[hs, c, 128:192], tok.t[hs, c, 128:192], False, True, [LM, tok], [bR])
                        self.act(l["RHSb"].t[:], pc[hp].t[:, 0:64], AF.Copy, [bR], [l["RHSb"]])
                    for hp in range(2):
                        l = L[hp]
                        TTs = l["TTs"][par]
                        bR, bU, bY, bH = pcb[hp]
                        for h in range(2):
                            hs = slice(64 * h, 64 * h + 64)
                            self.mm(pc[hp].t[hs, 64:128], TTs.t[hs, c, :], l["RHSb"].t[hs, :], True, True, [TTs, l["RHSb"]], [bU])
                        self.cp(l["Ub"].t[:], pc[hp].t[:, 64:128], [bU], [l["Ub"]])
                    for hp in range(2 if self.dbg.get("rstop", 9) >= 5 else 0):
                        l = L[hp]
                        ART, LM, tok, GC = l["ART"][par], l["LM"][par], l["tok"][par], l["GC"][par]
                        bR, bU, bY, bH = pcb[hp]
                        for h in range(2):
                            hs = slice(64 * h, 64 * h + 64)
                            self.mm(pc[hp].t[hs, 192:256], tok.t[hs, c, 0:64], l["Ub"].t[hs, :], True, False, [tok, l["Ub"]], [bH])
                            self.mm(pc[hp].t[hs, 192:256], tok.t[hs, c, 64:128], tok.t[hs, c, 128:192], False, True, [tok], [bH])
                        for h in range(2 if self.dbg.get("rstop", 9) >= 6 else 0):
                            hs = slice(64 * h, 64 * h + 64)
                            self.mm(py[hp].t[hs, 0:64], l["Hb"].t[hs, :], ART.t[hs, c, 1, :], True, False, [l["Hb"], ART], [py[hp]])
                            self.mm(py[hp].t[hs, 0:64], l["Ub"].t[hs, :], LM.t[hs, c, 64:128], False, False, [l["Ub"], LM], [py[hp]])
                            self.mm(py[hp].t[hs, 0:64], tok.t[hs, c, 128:192], LM.t[hs, c, 192:256], False, True, [tok, LM], [py[hp]])
                        self.stt(l["H"].t[:], l["H"].t[:], GC.t[:, c:c + 1], pc[hp].t[:, 192:256], ALU.mult, ALU.add, [l["H"], GC, bH], [l["H"]])
                        if self.dbg.get("rstop", 9) >= 6:
                            self.act(l["Y"][par].t[:, cs], py[hp].t[:, 0:64], AF.Copy, [py[hp]], [l["Y"][par]])
                        self.cp(l["Hb"].t[:], l["H"].t[:], [l["H"]], [l["Hb"]], eng="pool")
                if t0 >= CTX:
                    for hp in range(2):
                        self.dma(self.Y_d[d, hp, :, t0 - CTX:t0 - CTX + Ln], L[hp]["Y"][par].t[:, 0:Ln], [L[hp]["Y"][par]], [])
        S.flush()
        A.close()

    def phase_hgrn(self):
        nc, S = self.nc, self.S
        A = Arena(nc)
        f32t = lambda n, s=(128, 512): A.sb(n, s, F32)
        L = []
        for j in range(2):
            l = {}
            for n in ["pq", "pf", "pi", "P", "X2", "E1"]:
                l[n] = f32t(n)
            for n in ["KT", "Kh", "Vb"]:
                l[n] = A.sb(n, [128, 512], BF16)
            l["QT"] = [A.sb("QT", [128, 512], BF16) for _ in range(2)]
            l["Sc"] = [A.sb("Sc", [64, 8, 64], BF16) for _ in range(2)]
            l["tok"] = [A.sb("tokh", [64, 8, 256], BF16) for _ in range(2)]
            l["GC"] = [A.sb("GC", [128, 8], F32) for _ in range(2)]
            l["O"] = [f32t("O") for _ in range(2)]
            l["S"] = A.sb("S", [128, 128], F32)
            l["Sb"] = A.sb("Sb", [128, 128], BF16)
            L.append(l)
        ph_t = [A.ps("ph", [128, 512]) for _ in range(2)]
        ptp = A.ps("pth", [128, 1024], BF16)
        pcc = [A.ps("pcc", [128, 512]) for _ in range(2)]
        pcs = [A.ps("pcs", [128, 512]) for _ in range(2)]
        pcb = [[Buf("pcO"), Buf("pcS")] for _ in range(2)]
        segs = _segs()
        if self.dbg.get("nseg"):
            segs = segs[:self.dbg["nseg"]]
        nph = 0
        for d in range(2):
            mk = self.C("mkh%d" % d, 64, 0, 64)
            for j in range(2):
                self.memset(L[j]["S"].t[:], 0.0, [L[j]["S"]])
                self.memset(L[j]["Sb"].t[:], 0.0, [L[j]["Sb"]])
            order = segs if d == 0 else [segs[0]] + segs[:0:-1]
            for si, (t0, Ln) in enumerate(order):
                nch = Ln // 64
                par = si % 2
                for j in range(2):
                    l = L[j]
                    v = lambda n: l[n].t[:, 0:Ln]
                    v3 = lambda n: l[n].t[:, 0:Ln].rearrange("p (c t) -> p c t", t=64)
                    for n, ct in [("pq", 12 + j), ("pf", 14 + 2 * d + j), ("pi", 18 + j)]:
                        self.dma(v(n), self.P_d[ct, :, t0:t0 + Ln], [self.Pb[ct]], [l[n]])
                    self.act(v("pq"), v("pq"), AF.Silu, [l["pq"]], [l["pq"]])
                    self.act(v("pf"), v("pf"), AF.Sigmoid, [l["pf"]], [l["pf"]])
                    self.ts(v("pf"), v("pf"), self.SM(90 + d * 2 + j), self.SM(86 + d * 2 + j), ALU.mult, ALU.add,
                            [l["pf"], self.sm], [l["pf"]])
                    self.act(v("E1"), v("pf"), AF.Ln, [l["pf"]], [l["E1"]])
                    self.ts(v("pf"), v("pf"), -1.0, 1.0, ALU.mult, ALU.add, [l["pf"]], [l["pf"]], eng="pool")
                    S.op("dve", lambda e, o=v("P"), m=self.cst.t[:, CS["reset"]:CS["reset"] + Ln], s_=v("E1"):
                         e.tensor_tensor_scan(out=o, data0=m, data1=s_, initial=0.0, op0=ALU.mult, op1=ALU.add),
                         [self.cst, l["E1"]], [l["P"]])
                    GC = l["GC"][par]
                    self.act(GC.t[:, 0:nch], v3("P")[:, :, 63], AF.Exp, [l["P"]], [GC])
                    self.tt(v3("X2"), v3("P")[:, :, 63:64].to_broadcast([128, nch, 64]), v3("P"), ALU.subtract, [l["P"]], [l["X2"]])
                    if d == 0:
                        qi, qt = "P", "X2"
                    else:
                        self.tt(v("P"), v("P"), v("E1"), ALU.subtract, [l["P"], l["E1"]], [l["P"]], eng="pool")
                        self.tt(v("X2"), v("X2"), v("E1"), ALU.add, [l["X2"], l["E1"]], [l["X2"]])
                        qi, qt = "X2", "P"
                    self.act(v("E1"), v(qi), AF.Exp, [l[qi]], [l["E1"]])
                    self.act(v(qi), v(qi), AF.Exp, [l[qi]], [l[qi]], scale=-1.0)
                    self.act(v(qt), v(qt), AF.Exp, [l[qt]], [l[qt]])
                    QT = l["QT"][par]
                    self.tt(QT.t[:, 0:Ln], v("pq"), v("E1"), ALU.mult, [l["pq"], l["E1"]], [QT])
                    self.tt(v("KT"), v("pf"), v(qi), ALU.mult, [l["pf"], l[qi]], [l["KT"]], eng="pool")
                    self.tt(v("Kh"), v("pf"), v(qt), ALU.mult, [l["pf"], l[qt]], [l["Kh"]])
                    self.cp(v("Vb"), v("pi"), [l["pi"]], [l["Vb"]], eng="pool")
                    Sc, tok = l["Sc"][par], l["tok"][par]
                    for c in range(nch if self.dbg.get("hstop", 9) >= 2 else 0):
                        cs = slice(c * 64, (c + 1) * 64)
                        p = ph_t[nph % 2]; nph += 1
                        self.mm(p.t[0:64, 0:64], l["KT"].t[:, cs], QT.t[:, cs], True, True, [l["KT"], QT], [p])
                        self.tt(Sc.t[:, c, :], p.t[0:64, 0:64], mk, ALU.mult, [p, self.cst], [Sc])
                        self.tr(ptp.t[0:64, 0:128], l["Kh"].t[:, cs], self.identb.t[:], [l["Kh"], self.identb], [ptp])
                        self.tr(ptp.t[0:64, 128:256], l["Vb"].t[:, cs], self.identb.t[:], [l["Vb"], self.identb], [ptp], acc=True)
                        self.act(tok.t[:, c, :], ptp.t[0:64, 0:256], AF.Copy, [ptp], [tok])
                corder = list(range(nch)) if d == 0 else list(range(nch - 1, -1, -1))
                if self.dbg.get("hstop", 9) < 3:
                    corder = []
                for c in corder:
                    cs = slice(c * 64, (c + 1) * 64)
                    for j in range(2):
                        l = L[j]
                        QT, Sc, tok, GC = l["QT"][par], l["Sc"][par], l["tok"][par], l["GC"][par]
                        bO, bS = pcb[j]
                        self.mm(pcc[j].t[:, 0:64], tok.t[:, c, 128:256], Sc.t[:, c, :], True, False, [tok, Sc], [bO])
                        self.mm(pcc[j].t[:, 0:64], l["Sb"].t[:], QT.t[:, cs], False, True, [l["Sb"], QT], [bO])
                        self.mm(pcs[j].t[:, 0:128], tok.t[:, c, 0:128], tok.t[:, c, 128:256], True, True, [tok], [bS])
                        self.act(l["O"][par].t[:, cs], pcc[j].t[:, 0:64], AF.Copy, [bO], [l["O"][par]])
                        self.stt(l["S"].t[:], l["S"].t[:], GC.t[:, c:c + 1], pcs[j].t[:, 0:128], ALU.mult, ALU.add, [l["S"], GC, bS], [l["S"]])
                        self.cp(l["Sb"].t[:], l["S"].t[:], [l["S"]], [l["Sb"]], eng="pool")
                if t0 >= CTX:
                    for j in range(2):
                        self.dma(self.Y_d[d, 2 + j, :, t0 - CTX:t0 - CTX + Ln], L[j]["O"][par].t[:, 0:Ln], [L[j]["O"][par]], [])
        S.flush()
        A.close()

    def phase_out(self):
        nc, S = self.nc, self.S
        A = Arena(nc)
        TL = 512
        g2b = A.sb("g2b", [128, 2, 256], BF16)
        self.dma(g2b.t[:], self.g2_c.rearrange("(k p) c -> p k c", p=128), [], [g2b], q="pool", sem=6)
        T = {}
        for n in ["yf", "yr", "cen", "sq", "rs", "mr", "mk", "mv", "gl0", "gl1", "acc"]:
            T[n] = [A.sb(n, [128, TL], F32) for _ in range(2)]
        T["sg"] = [A.sb("sg", [128, 2, TL], BF16) for _ in range(2)]
        T["hi"] = [A.sb("hi", [128, TL], BF16) for _ in range(2)]
        T["lo"] = [A.sb("lo", [128, TL], BF16) for _ in range(2)]
        T["ub"] = [A.sb("ub", [128, TL], BF16) for _ in range(2)]
        ps = [A.ps("po", [128, 512]) for _ in range(4)]
        nps = 0
        ntl = self.dbg.get("out_tiles", SEQ // TL)
        it = 0
        for ti in range(ntl):
            x0 = ti * TL
            tp = CTX + x0
            tg, off = x0 // TPC, x0 % TPC
            for lane in range(4):
                b = it % 2
                it += 1
                t = {n: T[n][b] for n in T}
                self.dma(t["yf"].t[:], self.Y_d[0, lane, :, x0:x0 + TL], [], [t["yf"]])
                self.dma(t["yr"].t[:], self.Y_d[1, lane, :, x0:x0 + TL], [], [t["yr"]])
                self.tt(t["yf"].t[:], t["yf"].t[:], t["yr"].t[:], ALU.add, [t["yf"], t["yr"]], [t["yf"]], eng="pool")
                if lane < 2:
                    hp = lane
                    for n, j in [("mr", hp), ("mk", 2 + hp), ("mv", 4 + hp), ("gl0", 10), ("gl1", 11)]:
                        self.dma(t[n].t[:], self.P_d[j, :, tp:tp + TL], [], [t[n]])
                    p = ps[nps % 4]; nps += 1
                    self.statmm(p.t[:], p, "bo64", t["yf"].t[:], t["yf"], t["hi"], t["lo"], TL)
                    self.stt(t["cen"].t[:], p.t[:], -1.0 / 64, t["yf"].t[:], ALU.mult, ALU.add, [p, t["yf"]], [t["cen"]])
                    self.tt(t["sq"].t[:], t["cen"].t[:], t["cen"].t[:], ALU.mult, [t["cen"]], [t["sq"]], eng="pool")
                    p = ps[nps % 4]; nps += 1
                    self.statmm(p.t[:], p, "bo64", t["sq"].t[:], t["sq"], t["hi"], t["lo"], TL)
                    self.act(t["rs"].t[:], p.t[:], AF.Sqrt, [p, self.sm], [t["rs"]], scale=1.0 / 64, bias=self.SM(96))
                    self.recip(t["rs"].t[:], t["rs"].t[:], [t["rs"]], [t["rs"]])
                    self.tt(t["cen"].t[:], t["cen"].t[:], t["rs"].t[:], ALU.mult, [t["cen"], t["rs"]], [t["cen"]])
                    self.ts(t["cen"].t[:], t["cen"].t[:], self.V("lng", hp), self.V("lnb", hp), ALU.mult, ALU.add,
                            [t["cen"], self.pv], [t["cen"]])
                    self.tt(t["mr"].t[:], t["mr"].t[:], t["mk"].t[:], ALU.mult, [t["mr"], t["mk"]], [t["mr"]], eng="pool")
                    self.ts(t["mr"].t[:], t["mr"].t[:], self.V("rk", hp), None, ALU.mult, None, [t["mr"], self.pv], [t["mr"]], eng="pool")
                    p = ps[nps % 4]; nps += 1
                    self.statmm(p.t[:], p, "bo64", t["mr"].t[:], t["mr"], t["hi"], t["lo"], TL)
                    self.tt(t["acc"].t[:], p.t[:], t["mv"].t[:], ALU.mult, [p, t["mv"]], [t["acc"]])
                    self.tt(t["acc"].t[:], t["acc"].t[:], t["cen"].t[:], ALU.add, [t["acc"], t["cen"]], [t["acc"]], eng="pool")
                    self.act(t["sg"].t[:, 0, :], t["gl0"].t[:], AF.Sigmoid, [t["gl0"]], [t["sg"]])
                    self.act(t["sg"].t[:, 1, :], t["gl1"].t[:], AF.Sigmoid, [t["gl1"]], [t["sg"]])
                    p = ps[nps % 4]; nps += 1
                    for k in range(2):
                        self.mm(p.t[:], g2b.t[:, k, hp * 128:(hp + 1) * 128], t["sg"].t[:, k, :], k == 0, k == 1, [g2b, t["sg"]], [p])
                    self.tt(t["ub"].t[:], p.t[:], t["acc"].t[:], ALU.mult, [p, t["acc"]], [t["ub"]])
                else:
                    j = lane - 2
                    self.dma(t["mr"].t[:], self.P_d[20 + j, :, tp:tp + TL], [], [t["mr"]])
                    self.tt(t["sq"].t[:], t["yf"].t[:], t["yf"].t[:], ALU.mult, [t["yf"]], [t["sq"]], eng="pool")
                    p = ps[nps % 4]; nps += 1
                    self.statmm(p.t[:], p, "ones", t["sq"].t[:], t["sq"], t["hi"], t["lo"], TL)
                    self.act(t["rs"].t[:], p.t[:], AF.Sqrt, [p, self.sm], [t["rs"]], scale=1.0 / 128, bias=self.SM(94))
                    self.recip(t["rs"].t[:], t["rs"].t[:], [t["rs"]], [t["rs"]])
                    self.tt(t["cen"].t[:], t["yf"].t[:], t["rs"].t[:], ALU.mult, [t["yf"], t["rs"]], [t["cen"]])
                    self.ts(t["cen"].t[:], t["cen"].t[:], self.V("hng"), None, ALU.mult, None, [t["cen"], self.pv], [t["cen"]], eng="pool")
                    self.act(t["mr"].t[:], t["mr"].t[:], AF.Silu, [t["mr"]], [t["mr"]])
                    self.tt(t["ub"].t[:], t["cen"].t[:], t["mr"].t[:], ALU.mult, [t["cen"], t["mr"]], [t["ub"]])
                self.dma(self.u_d[tg, lane * 128:(lane + 1) * 128, off:off + TL], t["ub"].t[:], [t["ub"]], [])
        S.flush()
        A.close()

    def phase_gather(self, names):
        S = self.S
        for n in names:
            src, tmp, dst = self.gsrc[n]
            b = Buf("g_" + n)
            self.dma(tmp, src, [], [b])
            S.op("pool", lambda e, tmp=tmp, dst=dst: e.collective_compute("AllGather", ALU.bypass, replica_groups=[list(range(NCORE))],
                                                                          ins=[tmp], outs=[dst]), [b], [], dma=9, inc=1)
        S.flush()

    def phase_stage2(self):
        nc, S = self.nc, self.S
        B = self.bufs
        S.op("pool", lambda e: e.collective_compute("AllGather", ALU.bypass, replica_groups=[list(range(NCORE))],
                                                    ins=[self.u_d.rearrange("g c t -> (g c) t")],
                                                    outs=[self.u_all.rearrange("g c t -> (g c) t")]), [], [B["u_all"]], dma=9, inc=1)
        S.flush()
        nhalf = self.dbg.get("nhalf", 2)
        nfb = self.dbg.get("nfb", DFF // 256)
        TL = 512

        def mof(kt):
            return (kt // 2) * 4 + kt % 2 if kt < 16 else ((kt - 16) // 2) * 4 + 2 + kt % 2
        for hf in range(nhalf):
            A = Arena(nc)
            uT = A.sb("uT", [128, 32, TL], BF16)
            uxT = A.sb("uxT", [128, 32, TL], F32)
            ps = [A.ps("p2", [128, 512]) for _ in range(6)]
            pss = A.ps("pss", [128, 512])
            nps = 0
            x1b = [Buf("x1b%d" % i) for i in range(32)]

            pidc = {}

            def ld_u(e, r):
                if "v" not in pidc:
                    pidc["v"] = e.partition_id()
                pid = pidc["v"]
                src = self.u_all.rearrange("(r g) (q p) t -> p g r q t", g=NCORE, p=128)[:, bass.ds(pid, 1), r, :, hf * TL:(hf + 1) * TL]
                return e.dma_start(out=uT.t[:, r * 4:(r + 1) * 4, :].rearrange("p (o q) t -> p o q t", o=1), in_=src)
            for r in range(NCORE):
                S.op("pool", lambda e, r=r: ld_u(e, r), [B["u_all"]], [uT], dma=r % 6)
            wo = [A.sb("wo", [128, 32, 128], BF16) for _ in range(2)]
            sqt = [A.sb("sqt", [128, TL], F32) for _ in range(2)]
            shi = [A.sb("shi2", [128, TL], BF16) for _ in range(2)]
            slo = [A.sb("slo2", [128, TL], BF16) for _ in range(2)]
            rstd = A.sb("rstd", [128, TL], F32)
            tmp = [A.sb("tmp", [128, TL], F32) for _ in range(1)]
            x1t = [A.sb("x1t", [128, TL], F32) for _ in range(2)]
            xq = [A.sb("xq", [128, 4, 128], F32) for _ in range(1)]
            ot = [A.sb("ot", [128, 4, 128], F32) for _ in range(1)]
            for nt in range(32):
                w = wo[nt % 2]
                self.dma(w.t[:], self.w_out[:, nt * 128:(nt + 1) * 128].rearrange("(k p) n -> p k n", p=128), [], [w], q="pool", sem=6)
                p = ps[nps % 6]; nps += 1
                for kt in range(32):
                    self.mm(p.t[:], w.t[:, kt, :], uT.t[:, mof(kt), :], kt == 0, kt == 31, [w, uT], [p])
                self.act(uxT.t[:, nt, :], p.t[:], AF.Copy, [p], [uxT])
                sq = sqt[nt % 2]
                self.tt(sq.t[:], uxT.t[:, nt, :], uxT.t[:, nt, :], ALU.mult, [uxT], [sq], eng="pool")
                self.statmm(pss.t[:], pss, "ones", sq.t[:], sq, shi[nt % 2], slo[nt % 2], TL, start=(nt == 0), stop=(nt == 31))

            def mk_rstd():
                self.act(rstd.t[:], pss.t[:], AF.Sqrt, [pss, self.sm], [rstd], scale=1.0 / D, bias=self.SM(94))
                self.recip(rstd.t[:], rstd.t[:], [rstd], [rstd])
            mk_rstd()
            for nt in range(32):
                q = xq[0]
                self.dma(q.t[:], self.xs[hf * TL:(hf + 1) * TL, nt * 128:(nt + 1) * 128].rearrange("(b p) c -> p b c", p=128), [], [q])
                p = ps[nps % 6]; nps += 1
                for b in range(4):
                    self.tr(p.t[:, b * 128:(b + 1) * 128], q.t[:, b, :], self.C("ident"), [q, self.cst], [p], acc=(b != 0))
                t = tmp[0]
                self.stt(t.t[:], uxT.t[:, nt, :], self.MV(4, nt), rstd.t[:], ALU.mult, ALU.mult, [uxT, self.modv, rstd], [t])
                x1 = x1t[nt % 2]
                self.tt(x1.t[:], p.t[:], t.t[:], ALU.add, [p, t], [x1])
                sq = sqt[nt % 2]
                self.act(sq.t[:], x1.t[:], AF.Square, [x1], [sq])
                self.statmm(pss.t[:], pss, "ones", sq.t[:], sq, shi[nt % 2], slo[nt % 2], TL, start=(nt == 0), stop=(nt == 31))
                self.dma(self.x1_d[hf, nt], x1.t[:], [x1], [x1b[nt]])
            mk_rstd()
            hT = uT
            for nt in range(32):
                x1 = x1t[nt % 2]
                self.dma(x1.t[:], self.x1_d[hf, nt], [x1b[nt]], [x1])
                t = tmp[0]
                self.stt(t.t[:], x1.t[:], self.MV(5, nt), rstd.t[:], ALU.mult, ALU.mult, [x1, self.modv, rstd], [t])
                self.ts(hT.t[:, nt, :], t.t[:], self.MV(6, nt), None, ALU.add, None, [t, self.modv], [hT], eng="pool")
            yacc = uxT
            w1 = [A.sb("w1", [128, 32, 256], BF16) for _ in range(1)]
            w2 = [A.sb("w2", [128, 2, D], BF16) for _ in range(2)]
            aT = [A.sb("aT", [128, 2, TL], BF16) for _ in range(2)]
            rl = [A.sb("rl", [128, TL], F32) for _ in range(2)]
            for fb in range(nfb):
                a1, a2, a = w1[0], w2[fb % 2], aT[fb % 2]
                for kg in range(4):
                    self.dma(a1.t[:, kg * 8:(kg + 1) * 8, :],
                             self.w_ff1[kg * 1024:(kg + 1) * 1024, fb * 256:(fb + 1) * 256].rearrange("(k p) n -> p k n", p=128),
                             [], [a1], q="pool", sem=7)
                for h2 in range(2):
                    self.dma(a2.t[:, :, h2 * 2048:(h2 + 1) * 2048],
                             self.w_ff2[fb * 256:(fb + 1) * 256, h2 * 2048:(h2 + 1) * 2048].rearrange("(k p) n -> p k n", p=128),
                             [], [a2], q="pool", sem=8)
                for ft in range(2):
                    p = ps[nps % 6]; nps += 1
                    for kt in range(32):
                        self.mm(p.t[:], a1.t[:, kt, ft * 128:(ft + 1) * 128], hT.t[:, kt, :], kt == 0, kt == 31, [a1, hT], [p])
                    r_ = rl[ft]
                    self.act(r_.t[:], p.t[:], AF.Relu, [p], [r_])
                    self.tt(a.t[:, ft, :], r_.t[:], r_.t[:], ALU.mult, [r_], [a], eng="pool")
                for nt in range(32):
                    p = ps[nps % 6]; nps += 1
                    for ft in range(2):
                        self.mm(p.t[:], a2.t[:, ft, nt * 128:(nt + 1) * 128], a.t[:, ft, :], ft == 0, ft == 1, [a2, a], [p])
                    if fb == 0:
                        self.act(yacc.t[:, nt, :], p.t[:], AF.Copy, [p], [yacc])
                    else:
                        self.tt(yacc.t[:, nt, :], p.t[:], yacc.t[:, nt, :], ALU.add, [p, yacc], [yacc])
            for nt in range(32):
                sq = sqt[nt % 2]
                self.act(sq.t[:], yacc.t[:, nt, :], AF.Square, [yacc], [sq])
                self.statmm(pss.t[:], pss, "ones", sq.t[:], sq, shi[nt % 2], slo[nt % 2], TL, start=(nt == 0), stop=(nt == 31))
            mk_rstd()
            for nt in range(32):
                x1 = x1t[nt % 2]
                self.dma(x1.t[:], self.x1_d[hf, nt], [x1b[nt]], [x1])
                t = tmp[0]
                self.stt(t.t[:], yacc.t[:, nt, :], self.MV(7, nt), rstd.t[:], ALU.mult, ALU.mult, [yacc, self.modv, rstd], [t])
                self.tt(x1.t[:], x1.t[:], t.t[:], ALU.add, [x1, t], [x1], eng="pool")
                p = ps[nps % 6]; nps += 1
                for b in range(4):
                    self.tr(p.t[:, b * 128:(b + 1) * 128], x1.t[:, b * 128:(b + 1) * 128], self.C("ident"), [x1, self.cst], [p], acc=(b != 0))
                o = ot[0]
                self.act(o.t[:].rearrange("p b c -> p (b c)"), p.t[:], AF.Copy, [p], [o])
                self.dma(self.out[hf * TL:(hf + 1) * TL, nt * 128:(nt + 1) * 128].rearrange("(b p) c -> p b c", p=128), o.t[:], [o], [])
            S.flush()
            A.close()


def _build(dbg=None):
    k = K2(dbg)
    return k


def kernel(**inputs):
    maps = _prep(inputs)
    k = K2(dict(DEBUG))
    k.phase_gather(["xall", "w_out", "w_ff1", "w_ff2"])
    k.phase0()
    k.phase1()
    k.phase_shift()
    k.phase_rwkv()
    k.phase_hgrn()
    k.phase_out()
    k.phase_stage2()
    res = run_bass_kernel_spmd(k.nc, maps, core_ids=list(range(NCORE)))
    out = np.concatenate([np.asarray(r["out"], np.float32) for r in res.results], 0)
    return out[None]
```

```python
import contextlib
import numpy as np
import concourse.bass as bass
import concourse.mybir as mybir
from concourse.bass_utils import run_bass_kernel_spmd

F32 = mybir.dt.float32
BF16 = mybir.dt.bfloat16
AF = mybir.ActivationFunctionType
ALU = mybir.AluOpType

D = 4096
SEQ = 8192
CTX = 256
NT = SEQ + CTX
NCORE = 8
TPC = SEQ // NCORE
DFF = 16384
CL = -0.6065306597126334
CW = [128] * 6 + [96] * 4 + [128] * 2 + [128] * 10
COFF = [int(v) for v in np.cumsum([0] + CW)]
NCOL = COFF[-1]
NA = 12
PV = {}
_o = 0
for _n, _w in [("gpre", 32), ("gpost", 32), ("gfpre", 32), ("gfpost", 32), ("mu", 12), ("w0", 4), ("a0", 4),
               ("kk", 2), ("ka", 2), ("rk", 2), ("lng", 2), ("lnb", 2), ("lbl", 8), ("hng", 1), ("cls", 4)]:
    PV[_n] = _o
    _o += _w
NPV = _o
CS = {}
_o = 0
for _n, _w in [("ident", 128), ("bo64", 128), ("ones", 128), ("mkr0", 320), ("mkr1", 320), ("mkh0", 64), ("mkh1", 64),
               ("reset", 512)]:
    CS[_n] = _o
    _o += _w
NCS = _o

DEBUG = {}


class Buf:
    __slots__ = ("name", "w", "r", "t")

    def __init__(self, name, t=None):
        self.name = name
        self.w = None
        self.r = []
        self.t = t


class Sched:
    ENGS = ("pe", "act", "dve", "pool", "sp")
    LIMIT = 10 ** 9
    CLEAR = False

    LAZY = (10,)

    def __init__(self, nc, stack, n_dma_sems=12):
        self.nc = nc
        self.n_dma = n_dma_sems
        self.csem = {e: stack.enter_context(nc.semaphore("s_" + e)) for e in self.ENGS}
        self.dsem = [stack.enter_context(nc.semaphore("d_%d" % i)) for i in range(n_dma_sems)]
        self.total = 0
        self.epoch = 0
        self._reset()

    def _reset(self):
        self.ops = {e: [] for e in self.ENGS}
        self.cnt = {e: 0 for e in self.ENGS}
        self.seen = {e: {} for e in self.ENGS}
        self.dma_tot = [0] * self.n_dma

    def _waits_for(self, eng, deps):
        waits = {}
        for tg in deps:
            if tg is None or tg[3] != self.epoch:
                continue
            if tg[0] == "c":
                if tg[1] == "pe" and eng == "pe":
                    continue
                key = ("c", tg[1])
            else:
                key = ("d", tg[1])
            if self.seen[eng].get(key, 0) >= tg[2]:
                continue
            if waits.get(key, 0) < tg[2]:
                waits[key] = tg[2]
        for k, v in waits.items():
            self.seen[eng][k] = v
        return list(waits.items())

    def op(self, eng, fn, reads=(), writes=(), dma=None, inc=16, acc=False):
        if max(self.cnt.values()) >= self.LIMIT or max(self.dma_tot) >= self.LIMIT:
            self.flush()
        deps = []
        for b in reads:
            deps.append(b.w)
        if not acc:
            for b in writes:
                deps.append(b.w)
                deps.extend(b.r)
        waits = self._waits_for(eng, deps)
        if dma is not None:
            self.dma_tot[dma] += inc
            tag = ("d", dma, self.dma_tot[dma], self.epoch)
        else:
            self.cnt[eng] += 1
            tag = ("c", eng, self.cnt[eng], self.epoch)
        self.ops[eng].append((waits, fn, tag, inc))
        for b in reads:
            if b.r and b.r[0][3] != self.epoch:
                b.r = []
            b.r.append(tag)
        for b in writes:
            b.w = tag
            b.r = []
        self.total += 1
        return tag

    def _sem(self, key):
        return self.csem[key[1]] if key[0] == "c" else self.dsem[key[1]]

    def flush(self):
        nc = self.nc
        fin = self._waits_for("sp", [("d", i, v, self.epoch) for i, v in enumerate(self.dma_tot) if v and i not in self.LAZY])
        ops = self.ops
        with nc.Block() as block:
            def run(engname, e):
                for waits, fn, tag, inc in ops[engname]:
                    for key, val in waits:
                        e.wait_ge(self._sem(key), val)
                    ins = fn(e)
                    if tag[0] == "c":
                        ins.then_inc(self.csem[tag[1]], 1)
                    else:
                        ins.then_inc(self.dsem[tag[1]], inc)
                if engname == "sp":
                    for key, val in fin:
                        e.wait_ge(self._sem(key), val)

            @block.tensor
            def _(e):
                run("pe", e)

            @block.scalar
            def _(e):
                run("act", e)

            @block.vector
            def _(e):
                run("dve", e)

            @block.gpsimd
            def _(e):
                run("pool", e)

            @block.sync
            def _(e):
                run("sp", e)
        if self.CLEAR:
            with nc.Block() as block:
                @block.sync
                def _(e):
                    for sh in list(self.csem.values()) + self.dsem:
                        e.sem_clear(sh)
            self.epoch += 1
            self._reset()
        else:
            self.ops = {e: [] for e in self.ENGS}
            for e in self.ENGS:
                for e2 in self.ENGS:
                    self.seen[e][("c", e2)] = self.cnt[e2]
                for i, v in enumerate(self.dma_tot):
                    if i not in self.LAZY:
                        self.seen[e][("d", i)] = v


class Arena:
    N = [0]

    def __init__(self, nc):
        self.nc = nc
        self.st = contextlib.ExitStack()

    def sb(self, name, shape, dt):
        Arena.N[0] += 1
        return Buf(name, self.st.enter_context(self.nc.sbuf_tensor("a%d_%s" % (Arena.N[0], name), list(shape), dt)))

    def ps(self, name, shape, dt=F32):
        Arena.N[0] += 1
        return Buf(name, self.st.enter_context(self.nc.psum_tensor("a%d_%s" % (Arena.N[0], name), list(shape), dt)))

    def close(self):
        self.st.close()


class K:
    def __init__(self, dbg=None):
        self.dbg = dbg or {}
        nc = self.nc = bass.Bass("TRN2", target_bir_lowering=False)
        self.stack = contextlib.ExitStack()
        self.S = Sched(nc, self.stack)
        dkf = lambda n: "ExternalOutput" if n in self.dbg.get("dump", ()) else "Internal"
        di = lambda n, s, dt=F32: nc.dram_tensor(n, list(s), dt, kind="ExternalInput").ap()
        full = not self.dbg.get("no_stage2")
        gi = lambda n, rows, cols, ok=True: (
            di(n + "_s", [rows // NCORE, cols] if ok else [8, 8]),
            nc.dram_tensor(n + "_t", [rows // NCORE, cols] if ok else [8, 8], F32, kind="Internal").ap(),
            nc.dram_tensor(n + "_g", [rows, cols] if ok else [64, 8], F32, addr_space="Local", kind="Internal").ap())
        self.gsrc = {"xall": gi("xall", NT, D), "w_out": gi("w_out", D, D, full), "w_ff1": gi("w_ff1", D, DFF, full),
                     "w_ff2": gi("w_ff2", DFF, D, full)}
        self.xall = self.gsrc["xall"][2]
        self.xs = di("xs", [TPC, D])
        self.w_in_c = di("w_in_c", [D, NCOL])
        self.w_ada_c = di("w_ada_c", [D, 3072])
        self.b_ada_c = di("b_ada_c", [1, 3072])
        self.c2T = di("c2T", [128, 32, 2])
        self.pvd = di("pv", [128, NPV])
        self.cstd = di("cst", [128, NCS])
        self.w2_c = di("w2_c", [2, 96, 256])
        self.a2_c = di("a2_c", [2, 96, 256])
        self.g2_c = di("g2_c", [256, 256])
        self.w_out, self.w_ff1, self.w_ff2 = self.gsrc["w_out"][2], self.gsrc["w_ff1"][2], self.gsrc["w_ff2"][2]
        self.out = nc.dram_tensor("out", [TPC, D], F32, kind="ExternalOutput").ap()
        self.mod_in = nc.dram_tensor("mod_in", [2, 3072], F32, kind="Internal").ap()
        self.mod_all = nc.dram_tensor("mod_all", [16, 3072], F32, addr_space="Local", kind="Internal").ap()
        self.P_d = nc.dram_tensor("P_d", [22, 128, NT], F32, kind=dkf("P_d")).ap()
        self.xnT_d = nc.dram_tensor("xnT_d", [32, 128, NT], BF16, kind="Internal").ap()
        self.Y_d = nc.dram_tensor("Y_d", [2, 4, 128, SEQ], F32, kind=dkf("Y_d")).ap()
        self.u_d = nc.dram_tensor("u_d", [NCORE, 512, TPC], BF16, kind=dkf("u_d")).ap()
        self.u_all = nc.dram_tensor("u_all", [NCORE * NCORE, 512, TPC], BF16, addr_space="Local", kind="Internal").ap()
        self.x1_d = nc.dram_tensor("x1_d", [2, 32, 128, 512], F32, kind=dkf("x1_d")).ap()
        self.bufs = {n: Buf(n) for n in ["mod_in", "mod_all", "P_d", "xnT_d", "Y_d", "u_d", "u_all", "x1_d", "out"]}
        self.Pb = [Buf("P_d%d" % j) for j in range(22)]
        sb = lambda n, s, dt: Buf(n, self.stack.enter_context(nc.sbuf_tensor("sb_" + n, list(s), dt)))
        self.cst = sb("cst", [128, NCS], F32)
        self.pv = sb("pvs", [128, NPV], F32)
        self.modv = sb("modv", [128, 8, 32], F32)
        self.sm = sb("sm", [128, 128], F32)
        self.identb = sb("identb", [128, 128], BF16)
        self.cstb = sb("cstb", [128, 256], BF16)
        self.ndma = 0

    def dma(self, out, in_, reads, writes, q="sp", sem=None):
        if sem is None:
            sem = self.ndma % 6
            self.ndma += 1
        self.S.op(q, lambda e: e.dma_start(out=out, in_=in_), reads, writes, dma=sem)

    def mm(self, out, lhsT, rhs, start, stop, reads, writes):
        self.S.op("pe", lambda e: e.matmul(out, lhsT=lhsT, rhs=rhs, start=start, stop=stop), reads, writes, acc=not start)

    def tr(self, out, in_, ident, reads, writes, acc=False):
        self.S.op("pe", lambda e: e.transpose(out=out, in_=in_, identity=ident), reads, writes, acc=acc)

    def act(self, out, in_, func, reads, writes, scale=None, bias=None, accum=None):
        kw = {}
        if scale is not None:
            kw["scale"] = scale
        if bias is not None:
            kw["bias"] = bias
        if accum is not None:
            kw["accum_out"] = accum
        self.S.op("act", lambda e: e.activation(out=out, in_=in_, func=func, **kw), reads, writes)

    def tt(self, out, in0, in1, op, reads, writes, eng="dve"):
        self.S.op(eng, lambda e: e.tensor_tensor(out=out, in0=in0, in1=in1, op=op), reads, writes)

    def ts(self, out, in0, s1, s2, op0, op1, reads, writes, eng="dve"):
        if op1 is None:
            self.S.op(eng, lambda e: e.tensor_scalar(out=out, in0=in0, scalar1=s1, scalar2=None, op0=op0), reads, writes)
        else:
            self.S.op(eng, lambda e: e.tensor_scalar(out=out, in0=in0, scalar1=s1, scalar2=s2, op0=op0, op1=op1), reads, writes)

    def stt(self, out, in0, scalar, in1, op0, op1, reads, writes):
        self.S.op("dve", lambda e: e.scalar_tensor_tensor(out=out, in0=in0, scalar=scalar, in1=in1, op0=op0, op1=op1), reads, writes)

    def cp(self, out, in_, reads, writes, eng="dve"):
        self.S.op(eng, lambda e: e.tensor_copy(out=out, in_=in_), reads, writes)

    def memset(self, ap, val, writes, eng="pool"):
        self.S.op(eng, lambda e: e.memset(ap, val), (), writes)

    def recip(self, out, in_, reads, writes):
        self.S.op("dve", lambda e: e.reciprocal(out=out, in_=in_), reads, writes)

    def statmm(self, out_ps, psbuf, wname, src_ap, srcbuf, hi, lo, n, start=True, stop=True):
        w = self.cstb.t[:, 0:128] if wname == "bo64" else self.cstb.t[:, 128:256]
        self.act(hi.t[:, 0:n], src_ap, AF.Copy, [srcbuf], [hi])
        self.tt(lo.t[:, 0:n], src_ap, hi.t[:, 0:n], ALU.subtract, [srcbuf, hi], [lo])
        self.mm(out_ps, w, hi.t[:, 0:n], start, False, [self.cstb, hi], [psbuf])
        self.mm(out_ps, w, lo.t[:, 0:n], False, stop, [self.cstb, lo], [psbuf])

    def C(self, name, w=None, p0=0, p1=128):
        o = CS[name]
        if w is None:
            w = {"ident": 128, "bo64": 128, "ones": 128, "mkr0": 320, "mkr1": 320, "mkh0": 64, "mkh1": 64, "reset": 512}[name]
        return self.cst.t[p0:p1, o:o + w]

    def V(self, name, i=0, p0=0, p1=128):
        o = PV[name] + i
        return self.pv.t[p0:p1, o:o + 1]

    def MV(self, v, t, p0=0, p1=128):
        return self.modv.t[p0:p1, v, t:t + 1]

    def SM(self, c, p0=0, p1=128):
        return self.sm.t[p0:p1, c:c + 1]

    def phase0(self):
        nc, S = self.nc, self.S
        A = Arena(nc)
        cst, pv, sm = self.cst, self.pv, self.sm
        self.dma(cst.t[:], self.cstd, [], [cst])
        self.dma(pv.t[:], self.pvd, [], [pv])
        self.cp(self.identb.t[:], self.C("ident"), [cst], [self.identb])
        self.cp(self.cstb.t[:, 0:128], self.C("bo64"), [cst], [self.cstb])
        self.cp(self.cstb.t[:, 128:256], self.C("ones"), [cst], [self.cstb])
        mu = pv.t[:, PV["mu"]:PV["mu"] + 12]
        self.ts(sm.t[:, 0:12], mu, -1.0, 1.0, ALU.mult, ALU.add, [pv], [sm])
        for j in range(4):
            self.ts(sm.t[:, 12 + 12 * j:24 + 12 * j], mu, self.V("cls", j), None, ALU.mult, None, [pv], [sm])
        self.tt(sm.t[:, 60:72], sm.t[:, 12:24], sm.t[:, 36:48], ALU.add, [sm], [sm])
        self.tt(sm.t[:, 72:84], sm.t[:, 24:36], sm.t[:, 48:60], ALU.add, [sm], [sm])
        self.ts(sm.t[:, 84:86], pv.t[:, PV["ka"]:PV["ka"] + 2], -1.0, 1.0, ALU.mult, ALU.add, [pv], [sm])
        lo = PV["lbl"]
        self.tt(sm.t[:, 86:90], pv.t[:, lo:lo + 4], pv.t[:, lo + 4:lo + 8], ALU.subtract, [pv], [sm])
        self.act(sm.t[:, 86:90], sm.t[:, 86:90], AF.Sigmoid, [sm], [sm])
        self.ts(sm.t[:, 90:94], sm.t[:, 86:90], -1.0, 1.0, ALU.mult, ALU.add, [sm], [sm])
        self.memset(sm.t[:, 94:95], 1e-6, [sm])
        self.memset(sm.t[:, 95:96], 1e-12, [sm])
        self.memset(sm.t[:, 96:97], 64e-5, [sm])
        silT = A.sb("silT", [128, 32, 2], F32)
        self.dma(silT.t[:], self.c2T, [], [silT])
        self.act(silT.t[:], silT.t[:], AF.Silu, [silT], [silT])
        wa = [A.sb("wa", [128, 8, 512], F32) for _ in range(2)]
        mp = [A.ps("mp", [128, 512]) for _ in range(2)]
        modrow = A.sb("modrow", [2, 3072], F32)
        brow = A.sb("brow", [2, 3072], F32)
        self.dma(brow.t[:], self.b_ada_c.partition_broadcast(2), [], [brow])
        n = 0
        for cc in range(6):
            for kg in range(4):
                w = wa[n % 2]
                n += 1
                self.dma(w.t[:], self.w_ada_c[kg * 1024:(kg + 1) * 1024, cc * 512:(cc + 1) * 512].rearrange("(k p) n -> p k n", p=128), [], [w])
                for k in range(8):
                    kt = kg * 8 + k
                    self.mm(mp[cc % 2].t[0:2, :], silT.t[:, kt, :], w.t[:, k, :], kt == 0, kt == 31, [silT, w], [mp[cc % 2]])
            self.tt(modrow.t[:, cc * 512:(cc + 1) * 512], mp[cc % 2].t[0:2, :], brow.t[:, cc * 512:(cc + 1) * 512], ALU.add,
                    [mp[cc % 2], brow], [modrow])
        B = self.bufs
        self.dma(self.mod_in, modrow.t[:], [modrow], [B["mod_in"]])
        S.op("pool", lambda e: e.collective_compute("AllGather", ALU.bypass, replica_groups=[list(range(NCORE))],
                                                    ins=[self.mod_in], outs=[self.mod_all]),
             [B["mod_in"]], [B["mod_all"]], dma=9, inc=1)
        modg = A.sb("modg", [16, 3072], F32)
        self.dma(modg.t[:], self.mod_all, [B["mod_all"]], [modg])
        pst = A.ps("pst", [128, 384])
        for tt in range(24):
            self.tr(pst.t[:, tt * 16:(tt + 1) * 16], modg.t[0:16, tt * 128:(tt + 1) * 128], self.C("ident", 16, 0, 16),
                    [modg, cst], [pst], acc=(tt != 0))
        modT = A.sb("modT", [128, 384], F32)
        self.cp(modT.t[:], pst.t[:], [pst], [modT])
        raw = A.sb("raw", [128, 8, 32], F32)
        m4 = modT.t[:].rearrange("p (t r w) -> p t r w", r=8, w=2)
        for i, (v, row) in enumerate([(0, 0), (1, 0), (2, 0), (3, 0), (4, 0), (5, 0), (0, 1), (1, 1)]):
            T = v * 32
            while T < v * 32 + 32:
                R = T // 24
                Te = min((R + 1) * 24, v * 32 + 32)
                self.cp(raw.t[:, i, T - v * 32:Te - v * 32], m4[:, T - R * 24:Te - R * 24, R, row], [modT], [raw])
                T = Te
        mv = self.modv
        g = lambda n: pv.t[:, PV[n]:PV[n] + 32]
        self.stt(mv.t[:, 0, :], raw.t[:, 1, :], 1.0, g("gpre"), ALU.add, ALU.mult, [raw, pv], [mv])
        self.cp(mv.t[:, 1, :], raw.t[:, 0, :], [raw], [mv])
        self.stt(mv.t[:, 2, :], raw.t[:, 7, :], 1.0, g("gpre"), ALU.add, ALU.mult, [raw, pv], [mv])
        self.cp(mv.t[:, 3, :], raw.t[:, 6, :], [raw], [mv])
        self.tt(mv.t[:, 4, :], raw.t[:, 2, :], g("gpost"), ALU.mult, [raw, pv], [mv])
        self.stt(mv.t[:, 5, :], raw.t[:, 4, :], 1.0, g("gfpre"), ALU.add, ALU.mult, [raw, pv], [mv])
        self.cp(mv.t[:, 6, :], raw.t[:, 3, :], [raw], [mv])
        self.tt(mv.t[:, 7, :], raw.t[:, 5, :], g("gfpost"), ALU.mult, [raw, pv], [mv])
        S.flush()
        A.close()

    def phase1(self):
        nc, S = self.nc, self.S
        TT = 256
        ntile = self.dbg.get("p1_tiles", NT // TT)
        for pas in range(2):
            A = Arena(nc)
            cols = list(range(0, NA)) if pas == 0 else list(range(NA, 22))
            c0, c1 = COFF[cols[0]], COFF[cols[-1] + 1]
            W = A.sb("W", [128, 32, c1 - c0], BF16)
            for kg in range(4):
                self.dma(W.t[:, kg * 8:(kg + 1) * 8, :],
                         self.w_in_c[kg * 1024:(kg + 1) * 1024, c0:c1].rearrange("(k p) n -> p k n", p=128), [], [W], q="pool", sem=6)
            xb = [A.sb("xb", [128, D], F32) for _ in range(2)]
            xr = A.sb("xr", [128, D], F32)
            junk = A.sb("junk", [128, D], BF16)
            st = [A.sb("st", [128, 4], F32) for _ in range(2)]
            xnT = [A.sb("xnT", [128, 32, TT], BF16) for _ in range(2)]
            stg = [A.sb("stg", [128, TT], F32) for _ in range(4)]
            ptr = [A.ps("ptr", [128, 512]) for _ in range(3)]
            pmm = [A.ps("pmm", [128, 512]) for _ in range(4)]
            nb = 0
            ne = 0
            for ti in range(ntile):
                t0 = ti * TT
                xn = xnT[ti % 2]
                if pas == 0:
                    for blk in range(TT // 128):
                        x = xb[nb % 2]
                        s = st[nb % 2]
                        nb += 1
                        tb = t0 + blk * 128
                        self.dma(x.t[:], self.xall[tb:tb + 128, :], [], [x])
                        self.act(junk.t[:], x.t[:], AF.Square, [x], [junk, s], accum=s.t[:, 0:1])
                        self.act(s.t[:, 1:2], s.t[:, 0:1], AF.Sqrt, [s, self.sm], [s], scale=1.0 / D, bias=self.SM(94))
                        self.recip(s.t[:, 2:3], s.t[:, 1:2], [s], [s])
                        self.act(xr.t[:], x.t[:], AF.Copy, [x, s], [xr], scale=s.t[:, 2:3])
                        vg, vs = (2, 3) if tb < CTX else (0, 1)
                        for kq in range(8):
                            p = ptr[kq % 3]
                            for k in range(4):
                                kt = kq * 4 + k
                                self.tr(p.t[:, k * 128:(k + 1) * 128], xr.t[:, kt * 128:(kt + 1) * 128], self.C("ident"),
                                        [xr, self.cst], [p], acc=(k != 0))
                            for k in range(4):
                                kt = kq * 4 + k
                                o = xn.t[:, kt, blk * 128:(blk + 1) * 128]
                                if ne % 2 == 0:
                                    self.act(o, p.t[:, k * 128:(k + 1) * 128], AF.Identity, [p, self.modv], [xn],
                                             scale=self.MV(vg, kt), bias=self.MV(vs, kt))
                                else:
                                    self.ts(o, p.t[:, k * 128:(k + 1) * 128], self.MV(vg, kt), self.MV(vs, kt), ALU.mult, ALU.add,
                                            [p, self.modv], [xn])
                                ne += 1
                    self.dma(self.xnT_d[:, :, t0:t0 + TT].rearrange("k p t -> p k t"), xn.t[:], [xn], [self.bufs["xnT_d"]])
                else:
                    self.dma(xn.t[:], self.xnT_d[:, :, t0:t0 + TT].rearrange("k p t -> p k t"), [self.bufs["xnT_d"]], [xn])
                for ji, j in enumerate(cols):
                    w = CW[j]
                    o0 = COFF[j] - c0
                    p = pmm[ji % 4]
                    for kt in range(32):
                        self.mm(p.t[0:w, 0:TT], W.t[:, kt, o0:o0 + w], xn.t[:, kt, :], kt == 0, kt == 31, [W, xn], [p])
                    sg = stg[ji % 4]
                    if ji % 2 == 0:
                        self.act(sg.t[0:w, :], p.t[0:w, 0:TT], AF.Copy, [p], [sg])
                    else:
                        self.cp(sg.t[0:w, :], p.t[0:w, 0:TT], [p], [sg])
                    self.dma(self.P_d[j, 0:w, t0:t0 + TT], sg.t[0:w, :], [sg], [self.Pb[j]])
            S.flush()
            A.close()

    def phase_shift(self):
        nc, S = self.nc, self.S
        A = Arena(nc)
        pb = [A.sb("pb", [128, NT], F32) for _ in range(2)]
        mb = [A.sb("mb", [128, NT], F32) for _ in range(2)]
        for j in range(NA):
            w = CW[j]
            p, m = pb[j % 2], mb[j % 2]
            self.dma(p.t[0:w, :], self.P_d[j, 0:w, :], [self.Pb[j]], [p])
            self.act(m.t[0:w, :], p.t[0:w, :], AF.Copy, [p, self.sm], [m], scale=self.SM(j, 0, w))
            cf = lambda q: self.SM(12 + 12 * q + j, 0, w)
            px = p.t[0:w, CTX:NT]
            mx = m.t[0:w, CTX:NT]
            p3 = px.rearrange("p (r c) -> p r c", c=64)
            m3 = mx.rearrange("p (r c) -> p r c", c=64)
            rw = [p, m, self.sm]
            self.stt(m3[:, :, 1:64], p3[:, :, 0:63], cf(0), m3[:, :, 1:64], ALU.mult, ALU.add, rw, [m])
            self.stt(m3[:, :, 0:63], p3[:, :, 1:64], cf(1), m3[:, :, 0:63], ALU.mult, ALU.add, rw, [m])
            self.stt(mx[:, 64:SEQ], px[:, 0:SEQ - 64], cf(2), mx[:, 64:SEQ], ALU.mult, ALU.add, rw, [m])
            self.stt(mx[:, 0:SEQ - 64], px[:, 64:SEQ], cf(3), mx[:, 0:SEQ - 64], ALU.mult, ALU.add, rw, [m])
            pc = p.t[0:w, 0:CTX]
            mc = m.t[0:w, 0:CTX]
            self.stt(mc[:, 1:CTX], pc[:, 0:CTX - 1], cf(4), mc[:, 1:CTX], ALU.mult, ALU.add, rw, [m])
            self.stt(mc[:, 0:CTX - 1], pc[:, 1:CTX], cf(5), mc[:, 0:CTX - 1], ALU.mult, ALU.add, rw, [m])
            self.dma(self.P_d[j, 0:w, :], m.t[0:w, :], [m], [self.Pb[j]])
        S.flush()
        A.close()


def _consts():
    c = np.zeros((128, NCS), np.float32)
    p = np.arange(128)
    c[:, CS["ident"]:CS["ident"] + 128] = np.eye(128, dtype=np.float32)
    c[:, CS["bo64"]:CS["bo64"] + 128] = (p[:, None] // 64 == p[None, :] // 64)
    c[:, CS["ones"]:CS["ones"] + 128] = 1.0
    s = (p % 64)[:, None]
    t = np.arange(64)[None, :]
    lt, le, gt, ge = (s < t), (s <= t), (s > t), (s >= t)
    c[:, CS["mkr0"]:CS["mkr0"] + 320] = np.concatenate([lt, le, lt, le, gt], 1)
    c[:, CS["mkr1"]:CS["mkr1"] + 320] = np.concatenate([gt, ge, gt, ge, lt], 1)
    c[:, CS["mkh0"]:CS["mkh0"] + 64] = le
    c[:, CS["mkh1"]:CS["mkh1"] + 64] = ge
    c[:, CS["reset"]:CS["reset"] + 512] = (np.arange(512) % 64 != 0)[None, :]
    return c


def _prep(inp):
    f = lambda k: np.asarray(inp[k], np.float32)
    x, ctx = f("x")[0], f("ctx")[0]
    xall = np.ascontiguousarray(np.concatenate([ctx, x], 0))
    c2 = np.stack([f("c")[0], f("c_ctx")], 0)
    c2T = np.ascontiguousarray(c2.reshape(2, 32, 128).transpose(2, 1, 0))
    w_in, w_ada, b_ada = f("w_in")[0], f("w_ada")[0], f("b_ada")[0]
    w_out, w_ff1, w_ff2 = f("w_out")[0], f("w_ff1")[0], f("w_ff2")[0]
    colT = lambda v: v.reshape(32, 128).T
    cst = _consts()
    p = np.arange(128)
    maps = []
    for c in range(NCORE):
        b = 256 * c
        H = 6784
        cols = np.concatenate([np.arange(b, b + 256), 2048 + np.arange(b, b + 256), 4096 + np.arange(b, b + 256),
                               np.arange(6144, 6528), np.arange(6528, 6784),
                               H + np.arange(b, b + 256), H + 2048 + np.arange(b, b + 256), H + 4096 + np.arange(b, b + 256),
                               H + 6144 + np.arange(b, b + 256), H + 8192 + np.arange(b, b + 256)])
        assert cols.shape[0] == NCOL
        pv = np.zeros((128, NPV), np.float32)
        pv[:, PV["gpre"]:PV["gpre"] + 32] = colT(f("g_mix_pre")[0])
        pv[:, PV["gpost"]:PV["gpost"] + 32] = colT(f("g_mix_post")[0])
        pv[:, PV["gfpre"]:PV["gfpre"] + 32] = colT(f("g_ffn_pre")[0])
        pv[:, PV["gfpost"]:PV["gfpost"] + 32] = colT(f("g_ffn_post")[0])
        mu = f("mu_shift")[0]
        for j in range(NA):
            pv[:CW[j], PV["mu"] + j] = mu[cols[COFF[j]:COFF[j + 1]]]
        for d in range(2):
            for hp in range(2):
                sl = slice(b + hp * 128, b + hp * 128 + 128)
                pv[:, PV["w0"] + d * 2 + hp] = f("w0")[0, d, sl]
                pv[:, PV["a0"] + d * 2 + hp] = f("a0")[0, d, sl]
        for hp in range(2):
            sl = slice(b + hp * 128, b + hp * 128 + 128)
            pv[:, PV["kk"] + hp] = f("k_k")[0, sl]
            pv[:, PV["ka"] + hp] = f("k_a")[0, sl]
            pv[:, PV["rk"] + hp] = f("r_k")[0].reshape(-1)[sl]
            pv[:, PV["lng"] + hp] = f("ln_x_g")[0, sl]
            pv[:, PV["lnb"] + hp] = f("ln_x_b")[0, sl]
        lbl = f("hgrn_lb_logits")
        for sl_ in range(2):
            for d in range(2):
                for h in range(2):
                    pv[:, PV["lbl"] + sl_ * 4 + d * 2 + h] = lbl[sl_, d, b + 128 * h:b + 128 * h + 128]
        pv[:, PV["hng"]] = f("hgrn_norm_g")[0]
        for j in range(4):
            pv[:, PV["cls"] + j] = (p % 4 == j)
        maps.append({
            "xall_s": np.ascontiguousarray(xall[(NT // NCORE) * c:(NT // NCORE) * (c + 1)]),
            "xs": np.ascontiguousarray(x[TPC * c:TPC * (c + 1)]),
            "w_in_c": np.ascontiguousarray(w_in[:, cols]),
            "w_ada_c": np.ascontiguousarray(w_ada[:, 3072 * c:3072 * (c + 1)]),
            "b_ada_c": np.ascontiguousarray(b_ada[None, 3072 * c:3072 * (c + 1)]),
            "c2T": c2T, "pv": pv, "cst": cst,
            "w2_c": np.ascontiguousarray(f("w2")[0][:, :, b:b + 256]),
            "a2_c": np.ascontiguousarray(f("a2")[0][:, :, b:b + 256]),
            "g2_c": np.ascontiguousarray(f("g2")[0][:, b:b + 256]),
            "w_out_s": np.ascontiguousarray(w_out[512 * c:512 * (c + 1)]),
            "w_ff1_s": np.ascontiguousarray(w_ff1[512 * c:512 * (c + 1)]),
            "w_ff2_s": np.ascontiguousarray(w_ff2[2048 * c:2048 * (c + 1)]),
        })
    return maps


def _segs():
    return [(0, CTX)] + [(CTX + 512 * i, 512) for i in range(SEQ // 512)]


class K2(K):
    def phase_rwkv(self):
        nc, S = self.nc, self.S
        A = Arena(nc)
        f32t = lambda n, s=(128, 512): A.sb(n, s, F32)
        w2b = A.sb("w2b", [96, 2, 256], BF16)
        a2b = A.sb("a2b", [96, 2, 256], BF16)
        self.dma(w2b.t[:], self.w2_c.rearrange("d r c -> r d c"), [], [w2b], q="pool", sem=6)
        self.dma(a2b.t[:], self.a2_c.rearrange("d r c -> r d c"), [], [a2b], q="pool", sem=6)
        id2 = A.sb("id2", [128, 64], BF16)
        self.tt(id2.t[:], self.C("ident", 64), self.cst.t[:, CS["ident"] + 64:CS["ident"] + 128], ALU.add, [self.cst], [id2])
        L = []
        for hp in range(2):
            l = {}
            for n in ["mr", "mk", "mv", "mwl", "mal", "kk", "T1", "SG", "AD", "P", "X1", "X2", "E1"]:
                l[n] = f32t(n)
            l["th"] = A.sb("th", [128, 512], BF16)
            l["malb"] = A.sb("malb", [128, 512], BF16)
            l["shi"] = A.sb("shi", [128, 512], BF16)
            l["slo"] = A.sb("slo", [128, 512], BF16)
            for n in ["BT", "KT", "Bh", "Kh", "Vb"]:
                l[n] = A.sb(n, [128, 512], BF16)
            l["ART"] = [A.sb("ART", [128, 8, 2, 64], BF16) for _ in range(2)]
            l["LM"] = [A.sb("LM", [128, 8, 320], BF16) for _ in range(2)]
            l["TTs"] = [A.sb("TTs", [128, 8, 64], BF16) for _ in range(2)]
            l["tok"] = [A.sb("tok", [128, 8, 192], BF16) for _ in range(2)]
            l["GC"] = [A.sb("GC", [128, 8], F32) for _ in range(2)]
            l["Y"] = [f32t("Y") for _ in range(2)]
            l["TTw"] = [A.sb("TTw", [128, 64], BF16) for _ in range(3)]
            l["XZ"] = [A.sb("XZ", [128, 128], BF16) for _ in range(3)]
            l["H"] = A.sb("H", [128, 64], F32)
            l["Hb"] = A.sb("Hb", [128, 64], BF16)
            l["RHSb"] = A.sb("RHSb", [128, 64], BF16)
            l["Ub"] = A.sb("Ub", [128, 64], BF16)
            L.append(l)
        pf = [A.ps("pf", [128, 512]) for _ in range(1)]
        pa_t = [A.ps("pa", [128, 512]) for _ in range(2)]
        py = [A.ps("py", [128, 512]) for _ in range(2)]
        pa = [(t, t, t, t) for t in pa_t]
        ptp = A.ps("pt", [128, 1024], BF16)
        pc = [A.ps("pc", [128, 512]) for _ in range(2)]
        pcb = [[pc[i], pc[i], pc[i], pc[i]] for i in range(2)]
        segs = _segs()
        if self.dbg.get("nseg"):
            segs = segs[:self.dbg["nseg"]]
        npf = 0
        npa = 0
        nw = 0
        for d in range(2):
            mk = self.C("mkr%d" % d)
            for hp in range(2):
                self.memset(L[hp]["H"].t[:], 0.0, [L[hp]["H"]])
                self.memset(L[hp]["Hb"].t[:], 0.0, [L[hp]["Hb"]])
            order = segs if d == 0 else [segs[0]] + segs[:0:-1]
            for si, (t0, Ln) in enumerate(order):
                nch = Ln // 64
                par = si % 2
                for hp in range(2):
                    l = L[hp]
                    v = lambda n: l[n].t[:, 0:Ln]
                    v3 = lambda n: l[n].t[:, 0:Ln].rearrange("p (c t) -> p c t", t=64)
                    for n, j in [("mr", 0 + hp), ("mk", 2 + hp), ("mv", 4 + hp)]:
                        self.dma(v(n), self.P_d[j, :, t0:t0 + Ln], [self.Pb[j]], [l[n]])
                    self.dma(l["mwl"].t[0:96, 0:Ln], self.P_d[6 + d, 0:96, t0:t0 + Ln], [self.Pb[6 + d]], [l["mwl"]])
                    self.dma(l["mal"].t[0:96, 0:Ln], self.P_d[8 + d, 0:96, t0:t0 + Ln], [self.Pb[8 + d]], [l["mal"]])
                    self.act(l["th"].t[0:96, 0:Ln], l["mwl"].t[0:96, 0:Ln], AF.Tanh, [l["mwl"]], [l["th"]])
                    p = pf[0]; npf += 1
                    self.mm(p.t[:, 0:Ln], w2b.t[:, d, hp * 128:(hp + 1) * 128], l["th"].t[0:96, 0:Ln], True, True, [w2b, l["th"]], [p])
                    self.act(v("SG"), p.t[:, 0:Ln], AF.Sigmoid, [p, self.pv], [l["SG"]], bias=self.V("w0", d * 2 + hp))
                    self.cp(l["malb"].t[0:96, 0:Ln], l["mal"].t[0:96, 0:Ln], [l["mal"]], [l["malb"]], eng="pool")
                    p = pf[0]; npf += 1
                    self.mm(p.t[:, 0:Ln], a2b.t[:, d, hp * 128:(hp + 1) * 128], l["malb"].t[0:96, 0:Ln], True, True, [a2b, l["malb"]], [p])
                    self.act(v("AD"), p.t[:, 0:Ln], AF.Sigmoid, [p, self.pv], [l["AD"]], bias=self.V("a0", d * 2 + hp))
                    self.ts(v("kk"), v("mk"), self.V("kk", hp), None, ALU.mult, None, [l["mk"], self.pv], [l["kk"]])
                    self.tt(v("T1"), v("kk"), v("kk"), ALU.mult, [l["kk"]], [l["T1"]], eng="pool")
                    p = pf[0]; npf += 1
                    self.statmm(p.t[:, 0:Ln], p, "bo64", v("T1"), l["T1"], l["shi"], l["slo"], Ln)
                    self.act(v("T1"), p.t[:, 0:Ln], AF.Sqrt, [p, self.sm], [l["T1"]], bias=self.SM(95))
                    self.recip(v("T1"), v("T1"), [l["T1"]], [l["T1"]])
                    self.tt(v("kk"), v("kk"), v("T1"), ALU.mult, [l["kk"], l["T1"]], [l["kk"]])
                    self.ts(v("T1"), v("AD"), self.V("ka", hp), self.SM(84 + hp), ALU.mult, ALU.add, [l["AD"], self.pv, self.sm], [l["T1"]])
                    self.tt(v("T1"), v("T1"), v("mk"), ALU.mult, [l["T1"], l["mk"]], [l["T1"]])
                    self.tt(v("AD"), v("AD"), v("kk"), ALU.mult, [l["AD"], l["kk"]], [l["AD"]], eng="pool")
                    S.op("dve", lambda e, o=v("P"), m=self.cst.t[:, CS["reset"]:CS["reset"] + Ln], s=v("SG"):
                         e.tensor_tensor_scan(out=o, data0=m, data1=s, initial=0.0, op0=ALU.mult, op1=ALU.add),
                         [self.cst, l["SG"]], [l["P"]])
                    GC = l["GC"][par]
                    self.act(GC.t[:, 0:nch], v3("P")[:, :, 63], AF.Exp, [l["P"]], [GC], scale=CL)
                    self.tt(v("X1"), v("P"), v("SG"), ALU.subtract, [l["P"], l["SG"]], [l["X1"]], eng="pool")
                    self.tt(v3("X2"), v3("P")[:, :, 63:64].to_broadcast([128, nch, 64]), v3("P"), ALU.subtract, [l["P"]], [l["X2"]])
                    if d == 0:
                        qe, qt, qi = "X1", "X2", "P"
                    else:
                        qe, qt, qi = "X2", "X1", "SG"
                        self.tt(v("SG"), v("X2"), v("SG"), ALU.add, [l["X2"], l["SG"]], [l["SG"]])
                    self.act(v("E1"), v(qi), AF.Exp, [l[qi]], [l["E1"]], scale=CL)
                    self.act(v(qi), v(qi), AF.Exp, [l[qi]], [l[qi]], scale=-CL)
                    self.act(v(qe), v(qe), AF.Exp, [l[qe]], [l[qe]], scale=CL)
                    self.act(v(qt), v(qt), AF.Exp, [l[qt]], [l[qt]], scale=CL)
                    ART = l["ART"][par]
                    self.stt(ART.t[:, 0:nch, 0, :], v3("kk"), -1.0, v3(qe), ALU.mult, ALU.mult, [l["kk"], l[qe]], [ART])
                    self.tt(ART.t[:, 0:nch, 1, :], v3("mr"), v3("E1"), ALU.mult, [l["mr"], l["E1"]], [ART], eng="pool")
                    self.tt(v("BT"), v("AD"), v(qi), ALU.mult, [l["AD"], l[qi]], [l["BT"]])
                    self.tt(v("KT"), v("T1"), v(qi), ALU.mult, [l["T1"], l[qi]], [l["KT"]], eng="pool")
                    self.tt(v("Bh"), v("AD"), v(qt), ALU.mult, [l["AD"], l[qt]], [l["Bh"]])
                    self.tt(v("Kh"), v("T1"), v(qt), ALU.mult, [l["T1"], l[qt]], [l["Kh"]], eng="pool")
                    self.cp(v("Vb"), v("mv"), [l["mv"]], [l["Vb"]], eng="pool")
                    LM, TTs, tok = l["LM"][par], l["TTs"][par], l["tok"][par]
                    rstop = self.dbg.get("rstop", 9)
                    for c in range(nch if rstop >= 2 else 0):
                        cs = slice(c * 64, (c + 1) * 64)
                        bank, bLM, bXZ, bTT = pa[npa % 2]; npa += 1
                        for h in range(2):
                            hs = slice(64 * h, 64 * h + 64)
                            art = ART.t[hs, c, :, :].rearrange("p a t -> p (a t)")
                            self.mm(bank.t[hs, 0:128], l["BT"].t[hs, cs], art, True, True, [l["BT"], ART], [bLM])
                            self.mm(bank.t[hs, 128:256], l["KT"].t[hs, cs], art, True, True, [l["KT"], ART], [bLM])
                            self.mm(bank.t[hs, 256:320], ART.t[hs, c, 0, :], l["BT"].t[hs, cs], True, True, [l["BT"], ART], [bLM])
                        self.tt(LM.t[:, c, :], bank.t[:, 0:320], mk, ALU.mult, [bLM, self.cst], [LM])
                        for h in range(2):
                            hs = slice(64 * h, 64 * h + 64)
                            ib = self.identb.t[hs, 64 * h:64 * h + 64]
                            self.tr(ptp.t[hs, 0:64], l["Bh"].t[hs, cs], ib, [l["Bh"], self.identb], [ptp])
                            self.tr(ptp.t[hs, 64:128], l["Kh"].t[hs, cs], ib, [l["Kh"], self.identb], [ptp], acc=True)
                            self.tr(ptp.t[hs, 128:192], l["Vb"].t[hs, cs], ib, [l["Vb"], self.identb], [ptp], acc=True)
                        self.act(tok.t[:, c, :], ptp.t[:, 0:192], AF.Copy, [ptp], [tok])
                        TT = l["TTw"][nw % 3]; nw += 1
                        self.tt(TT.t[:], LM.t[:, c, 0:64], id2.t[:], ALU.add, [LM, id2], [TT], eng="pool")
                        Xap = lambda hs_: LM.t[hs_, c, 256:320]
                        Zap = lambda hs_: LM.t[hs_, c, 0:64]
                        Xb = Zb = LM
                        for lev in range(5 if rstop >= 3 else 0):
                            for h in range(2):
                                hs = slice(64 * h, 64 * h + 64)
                                self.mm(bank.t[hs, 320:384], Zap(hs), Xap(hs), True, True, [Xb], [bXZ])
                                self.mm(bank.t[hs, 384:448], Xap(hs), Zap(hs), True, True, [Xb], [bXZ])
                            XZ = l["XZ"][nw % 3]; nw += 1
                            self.act(XZ.t[:], bank.t[:, 320:448], AF.Copy, [bXZ], [XZ])
                            Xap = lambda hs_, XZ=XZ: XZ.t[hs_, 0:64]
                            Zap = lambda hs_, XZ=XZ: XZ.t[hs_, 64:128]
                            Xb = XZ
                            for h in range(2):
                                hs = slice(64 * h, 64 * h + 64)
                                self.mm(bank.t[hs, 448:512], Xap(hs), TT.t[hs, :], True, True, [XZ, TT], [bTT])
                            if lev < 4:
                                TTn = l["TTw"][nw % 3]; nw += 1
                                self.tt(TTn.t[:], bank.t[:, 448:512], TT.t[:], ALU.add, [bTT, TT], [TTn])
                                TT = TTn
                            else:
                                self.tt(TTs.t[:, c, :], bank.t[:, 448:512], TT.t[:], ALU.add, [bTT, TT], [TTs])
                corder = list(range(nch)) if d == 0 else list(range(nch - 1, -1, -1))
                if self.dbg.get("rstop", 9) < 4:
                    corder = []
                for c in corder:
                    cs = slice(c * 64, (c + 1) * 64)
                    for hp in range(2):
                        l = L[hp]
                        ART, LM, TTs, tok = l["ART"][par], l["LM"][par], l["TTs"][par], l["tok"][par]
                        bR, bU, bY, bH = pcb[hp]
                        for h in range(2):
                            hs = slice(64 * h, 64 * h + 64)
                            self.mm(pc[hp].t[hs, 0:64], ART.t[hs, c, 0, :], l["Hb"].t[hs, :], True, False, [ART, l["Hb"]], [bR])
                            self.mm(pc[hp].t[hs, 0:64], LM.t[hs, c, 128:192], tok.t[hs, c, 128:192], False, True, [LM, tok], [bR])
                        self.act(l["RHSb"].t[:], pc[hp].t[:, 0:64], AF.Copy, [bR], [l["RHSb"]])
                    for hp in range(2):
                        l = L[hp]
                        TTs = l["TTs"][par]
                        bR, bU, bY, bH = pcb[hp]
                        for h in range(2):
                            hs = slice(64 * h, 64 * h + 64)
                            self.mm(pc[hp].t[hs, 64:128], TTs.t[hs, c, :], l["RHSb"].t[hs, :], True, True, [TTs, l["RHSb"]], [bU])
                        self.cp(l["Ub"].t[:], pc[hp].t[:, 64:128], [bU], [l["Ub"]])
                    for hp in range(2 if self.dbg.get("rstop", 9) >= 5 else 0):
                        l = L[hp]
                        ART, LM, tok, GC = l["ART"][par], l["LM"][par], l["tok"][par], l["GC"][par]
                        bR, bU, bY, bH = pcb[hp]
                        for h in range(2):
                            hs = slice(64 * h, 64 * h + 64)
                            self.mm(pc[hp].t[hs, 192:256], tok.t[hs, c, 0:64], l["Ub"].t[hs, :], True, False, [tok, l["Ub"]], [bH])
                            self.mm(pc[hp].t[hs, 192:256], tok.t[hs, c, 64:128], tok.t[hs, c, 128:192], False, True, [tok], [bH])
                        for h in range(2 if self.dbg.get("rstop", 9) >= 6 else 0):
                            hs = slice(64 * h, 64 * h + 64)
                            self.mm(py[hp].t[hs, 0:64], l["Hb"].t[hs, :], ART.t[hs, c, 1, :], True, False, [l["Hb"], ART], [py[hp]])
                            self.mm(py[hp].t[hs, 0:64], l["Ub"].t[hs, :], LM.t[hs, c, 64:128], False, False, [l["Ub"], LM], [py[hp]])
                            self.mm(py[hp].t[hs, 0:64], tok.t[hs, c, 128:192], LM.t[hs, c, 192:256], False, True, [tok, LM], [py[hp]])
                        self.stt(l["H"].t[:], l["H"].t[:], GC.t[:, c:c + 1], pc[hp].t[:, 192:256], ALU.mult, ALU.add, [l["H"], GC, bH], [l["H"]])
                        if self.dbg.get("rstop", 9) >= 6:
                            self.act(l["Y"][par].t[:, cs], py[hp].t[:, 0:64], AF.Copy, [py[hp]], [l["Y"][par]])
                        self.cp(l["Hb"].t[:], l["H"].t[:], [l["H"]], [l["Hb"]], eng="pool")
                if t0 >= CTX:
                    for hp in range(2):
                        self.dma(self.Y_d[d, hp, :, t0 - CTX:t0 - CTX + Ln], L[hp]["Y"][par].t[:, 0:Ln], [L[hp]["Y"][par]], [])
        S.flush()
        A.close()

    def phase_hgrn(self):
        nc, S = self.nc, self.S
        A = Arena(nc)
        f32t = lambda n, s=(128, 512): A.sb(n, s, F32)
        L = []
        for j in range(2):
            l = {}
            for n in ["pq", "pf", "pi", "P", "X2", "E1"]:
                l[n] = f32t(n)
            for n in ["KT", "Kh", "Vb"]:
                l[n] = A.sb(n, [128, 512], BF16)
            l["QT"] = [A.sb("QT", [128, 512], BF16) for _ in range(2)]
            l["Sc"] = [A.sb("Sc", [64, 8, 64], BF16) for _ in range(2)]
            l["tok"] = [A.sb("tokh", [64, 8, 256], BF16) for _ in range(2)]
            l["GC"] = [A.sb("GC", [128, 8], F32) for _ in range(2)]
            l["O"] = [f32t("O") for _ in range(2)]
            l["S"] = A.sb("S", [128, 128], F32)
            l["Sb"] = A.sb("Sb", [128, 128], BF16)
            L.append(l)
        ph_t = [A.ps("ph", [128, 512]) for _ in range(2)]
        ptp = A.ps("pth", [128, 1024], BF16)
        pcc = [A.ps("pcc", [128, 512]) for _ in range(2)]
        pcs = [A.ps("pcs", [128, 512]) for _ in range(2)]
        pcb = [[Buf("pcO"), Buf("pcS")] for _ in range(2)]
        segs = _segs()
        if self.dbg.get("nseg"):
            segs = segs[:self.dbg["nseg"]]
        nph = 0
        for d in range(2):
            mk = self.C("mkh%d" % d, 64, 0, 64)
            for j in range(2):
                self.memset(L[j]["S"].t[:], 0.0, [L[j]["S"]])
                self.memset(L[j]["Sb"].t[:], 0.0, [L[j]["Sb"]])
            order = segs if d == 0 else [segs[0]] + segs[:0:-1]
            for si, (t0, Ln) in enumerate(order):
                nch = Ln // 64
                par = si % 2
                for j in range(2):
                    l = L[j]
                    v = lambda n: l[n].t[:, 0:Ln]
                    v3 = lambda n: l[n].t[:, 0:Ln].rearrange("p (c t) -> p c t", t=64)
                    for n, ct in [("pq", 12 + j), ("pf", 14 + 2 * d + j), ("pi", 18 + j)]:
                        self.dma(v(n), self.P_d[ct, :, t0:t0 + Ln], [self.Pb[ct]], [l[n]])
                    self.act(v("pq"), v("pq"), AF.Silu, [l["pq"]], [l["pq"]])
                    self.act(v("pf"), v("pf"), AF.Sigmoid, [l["pf"]], [l["pf"]])
                    self.ts(v("pf"), v("pf"), self.SM(90 + d * 2 + j), self.SM(86 + d * 2 + j), ALU.mult, ALU.add,
                            [l["pf"], self.sm], [l["pf"]])
                    self.act(v("E1"), v("pf"), AF.Ln, [l["pf"]], [l["E1"]])
                    self.ts(v("pf"), v("pf"), -1.0, 1.0, ALU.mult, ALU.add, [l["pf"]], [l["pf"]], eng="pool")
                    S.op("dve", lambda e, o=v("P"), m=self.cst.t[:, CS["reset"]:CS["reset"] + Ln], s_=v("E1"):
                         e.tensor_tensor_scan(out=o, data0=m, data1=s_, initial=0.0, op0=ALU.mult, op1=ALU.add),
                         [self.cst, l["E1"]], [l["P"]])
                    GC = l["GC"][par]
                    self.act(GC.t[:, 0:nch], v3("P")[:, :, 63], AF.Exp, [l["P"]], [GC])
                    self.tt(v3("X2"), v3("P")[:, :, 63:64].to_broadcast([128, nch, 64]), v3("P"), ALU.subtract, [l["P"]], [l["X2"]])
                    if d == 0:
                        qi, qt = "P", "X2"
                    else:
                        self.tt(v("P"), v("P"), v("E1"), ALU.subtract, [l["P"], l["E1"]], [l["P"]], eng="pool")
                        self.tt(v("X2"), v("X2"), v("E1"), ALU.add, [l["X2"], l["E1"]], [l["X2"]])
                        qi, qt = "X2", "P"
                    self.act(v("E1"), v(qi), AF.Exp, [l[qi]], [l["E1"]])
                    self.act(v(qi), v(qi), AF.Exp, [l[qi]], [l[qi]], scale=-1.0)
                    self.act(v(qt), v(qt), AF.Exp, [l[qt]], [l[qt]])
                    QT = l["QT"][par]
                    self.tt(QT.t[:, 0:Ln], v("pq"), v("E1"), ALU.mult, [l["pq"], l["E1"]], [QT])
                    self.tt(v("KT"), v("pf"), v(qi), ALU.mult, [l["pf"], l[qi]], [l["KT"]], eng="pool")
                    self.tt(v("Kh"), v("pf"), v(qt), ALU.mult, [l["pf"], l[qt]], [l["Kh"]])
                    self.cp(v("Vb"), v("pi"), [l["pi"]], [l["Vb"]], eng="pool")
                    Sc, tok = l["Sc"][par], l["tok"][par]
                    for c in range(nch if self.dbg.get("hstop", 9) >= 2 else 0):
                        cs = slice(c * 64, (c + 1) * 64)
                        p = ph_t[nph % 2]; nph += 1
                        self.mm(p.t[0:64, 0:64], l["KT"].t[:, cs], QT.t[:, cs], True, True, [l["KT"], QT], [p])
                        self.tt(Sc.t[:, c, :], p.t[0:64, 0:64], mk, ALU.mult, [p, self.cst], [Sc])
                        self.tr(ptp.t[0:64, 0:128], l["Kh"].t[:, cs], self.identb.t[:], [l["Kh"], self.identb], [ptp])
                        self.tr(ptp.t[0:64, 128:256], l["Vb"].t[:, cs], self.identb.t[:], [l["Vb"], self.identb], [ptp], acc=True)
                        self.act(tok.t[:, c, :], ptp.t[0:64, 0:256], AF.Copy, [ptp], [tok])
                corder = list(range(nch)) if d == 0 else list(range(nch - 1, -1, -1))
                if self.dbg.get("hstop", 9) < 3:
                    corder = []
                for c in corder:
                    cs = slice(c * 64, (c + 1) * 64)
                    for j in range(2):
                        l = L[j]
                        QT, Sc, tok, GC = l["QT"][par], l["Sc"][par], l["tok"][par], l["GC"][par]
                        bO, bS = pcb[j]
                        self.mm(pcc[j].t[:, 0:64], tok.t[:, c, 128:256], Sc.t[:, c, :], True, False, [tok, Sc], [bO])
                        self.mm(pcc[j].t[:, 0:64], l["Sb"].t[:], QT.t[:, cs], False, True, [l["Sb"], QT], [bO])
                        self.mm(pcs[j].t[:, 0:128], tok.t[:, c, 0:128], tok.t[:, c, 128:256], True, True, [tok], [bS])
                        self.act(l["O"][par].t[:, cs], pcc[j].t[:, 0:64], AF.Copy, [bO], [l["O"][par]])
                        self.stt(l["S"].t[:], l["S"].t[:], GC.t[:, c:c + 1], pcs[j].t[:, 0:128], ALU.mult, ALU.add, [l["S"], GC, bS], [l["S"]])
                        self.cp(l["Sb"].t[:], l["S"].t[:], [l["S"]], [l["Sb"]], eng="pool")
                if t0 >= CTX:
                    for j in range(2):
                        self.dma(self.Y_d[d, 2 + j, :, t0 - CTX:t0 - CTX + Ln], L[j]["O"][par].t[:, 0:Ln], [L[j]["O"][par]], [])
        S.flush()
        A.close()

    def phase_out(self):
        nc, S = self.nc, self.S
        A = Arena(nc)
        TL = 512
        g2b = A.sb("g2b", [128, 2, 256], BF16)
        self.dma(g2b.t[:], self.g2_c.rearrange("(k p) c -> p k c", p=128), [], [g2b], q="pool", sem=6)
        T = {}
        for n in ["yf", "yr", "cen", "sq", "rs", "mr", "mk", "mv", "gl0", "gl1", "acc"]:
            T[n] = [A.sb(n, [128, TL], F32) for _ in range(2)]
        T["sg"] = [A.sb("sg", [128, 2, TL], BF16) for _ in range(2)]
        T["hi"] = [A.sb("hi", [128, TL], BF16) for _ in range(2)]
        T["lo"] = [A.sb("lo", [128, TL], BF16) for _ in range(2)]
        T["ub"] = [A.sb("ub", [128, TL], BF16) for _ in range(2)]
        ps = [A.ps("po", [128, 512]) for _ in range(4)]
        nps = 0
        ntl = self.dbg.get("out_tiles", SEQ // TL)
        it = 0
        for ti in range(ntl):
            x0 = ti * TL
            tp = CTX + x0
            tg, off = x0 // TPC, x0 % TPC
            for lane in range(4):
                b = it % 2
                it += 1
                t = {n: T[n][b] for n in T}
                self.dma(t["yf"].t[:], self.Y_d[0, lane, :, x0:x0 + TL], [], [t["yf"]])
                self.dma(t["yr"].t[:], self.Y_d[1, lane, :, x0:x0 + TL], [], [t["yr"]])
                self.tt(t["yf"].t[:], t["yf"].t[:], t["yr"].t[:], ALU.add, [t["yf"], t["yr"]], [t["yf"]], eng="pool")
                if lane < 2:
                    hp = lane
                    for n, j in [("mr", hp), ("mk", 2 + hp), ("mv", 4 + hp), ("gl0", 10), ("gl1", 11)]:
                        self.dma(t[n].t[:], self.P_d[j, :, tp:tp + TL], [], [t[n]])
                    p = ps[nps % 4]; nps += 1
                    self.statmm(p.t[:], p, "bo64", t["yf"].t[:], t["yf"], t["hi"], t["lo"], TL)
                    self.stt(t["cen"].t[:], p.t[:], -1.0 / 64, t["yf"].t[:], ALU.mult, ALU.add, [p, t["yf"]], [t["cen"]])
                    self.tt(t["sq"].t[:], t["cen"].t[:], t["cen"].t[:], ALU.mult, [t["cen"]], [t["sq"]], eng="pool")
                    p = ps[nps % 4]; nps += 1
                    self.statmm(p.t[:], p, "bo64", t["sq"].t[:], t["sq"], t["hi"], t["lo"], TL)
                    self.act(t["rs"].t[:], p.t[:], AF.Sqrt, [p, self.sm], [t["rs"]], scale=1.0 / 64, bias=self.SM(96))
                    self.recip(t["rs"].t[:], t["rs"].t[:], [t["rs"]], [t["rs"]])
                    self.tt(t["cen"].t[:], t["cen"].t[:], t["rs"].t[:], ALU.mult, [t["cen"], t["rs"]], [t["cen"]])
                    self.ts(t["cen"].t[:], t["cen"].t[:], self.V("lng", hp), self.V("lnb", hp), ALU.mult, ALU.add,
                            [t["cen"], self.pv], [t["cen"]])
                    self.tt(t["mr"].t[:], t["mr"].t[:], t["mk"].t[:], ALU.mult, [t["mr"], t["mk"]], [t["mr"]], eng="pool")
                    self.ts(t["mr"].t[:], t["mr"].t[:], self.V("rk", hp), None, ALU.mult, None, [t["mr"], self.pv], [t["mr"]], eng="pool")
                    p = ps[nps % 4]; nps += 1
                    self.statmm(p.t[:], p, "bo64", t["mr"].t[:], t["mr"], t["hi"], t["lo"], TL)
                    self.tt(t["acc"].t[:], p.t[:], t["mv"].t[:], ALU.mult, [p, t["mv"]], [t["acc"]])
                    self.tt(t["acc"].t[:], t["acc"].t[:], t["cen"].t[:], ALU.add, [t["acc"], t["cen"]], [t["acc"]], eng="pool")
                    self.act(t["sg"].t[:, 0, :], t["gl0"].t[:], AF.Sigmoid, [t["gl0"]], [t["sg"]])
                    self.act(t["sg"].t[:, 1, :], t["gl1"].t[:], AF.Sigmoid, [t["gl1"]], [t["sg"]])
                    p = ps[nps % 4]; nps += 1
                    for k in range(2):
                        self.mm(p.t[:], g2b.t[:, k, hp * 128:(hp + 1) * 128], t["sg"].t[:, k, :], k == 0, k == 1, [g2b, t["sg"]], [p])
                    self.tt(t["ub"].t[:], p.t[:], t["acc"].t[:], ALU.mult, [p, t["acc"]], [t["ub"]])
                else:
                    j = lane - 2
                    self.dma(t["mr"].t[:], self.P_d[20 + j, :, tp:tp + TL], [], [t["mr"]])
                    self.tt(t["sq"].t[:], t["yf"].t[:], t["yf"].t[:], ALU.mult, [t["yf"]], [t["sq"]], eng="pool")
                    p = ps[nps % 4]; nps += 1
                    self.statmm(p.t[:], p, "ones", t["sq"].t[:], t["sq"], t["hi"], t["lo"], TL)
                    self.act(t["rs"].t[:], p.t[:], AF.Sqrt, [p, self.sm], [t["rs"]], scale=1.0 / 128, bias=self.SM(94))
                    self.recip(t["rs"].t[:], t["rs"].t[:], [t["rs"]], [t["rs"]])
                    self.tt(t["cen"].t[:], t["yf"].t[:], t["rs"].t[:], ALU.mult, [t["yf"], t["rs"]], [t["cen"]])
                    self.ts(t["cen"].t[:], t["cen"].t[:], self.V("hng"), None, ALU.mult, None, [t["cen"], self.pv], [t["cen"]], eng="pool")
                    self.act(t["mr"].t[:], t["mr"].t[:], AF.Silu, [t["mr"]], [t["mr"]])
                    self.tt(t["ub"].t[:], t["cen"].t[:], t["mr"].t[:], ALU.mult, [t["cen"], t["mr"]], [t["ub"]])
                self.dma(self.u_d[tg, lane * 128:(lane + 1) * 128, off:off + TL], t["ub"].t[:], [t["ub"]], [])
        S.flush()
        A.close()

    def phase_gather(self, names):
        S = self.S
        for n in names:
            src, tmp, dst = self.gsrc[n]
            b = Buf("g_" + n)
            self.dma(tmp, src, [], [b])
            S.op("pool", lambda e, tmp=tmp, dst=dst: e.collective_compute("AllGather", ALU.bypass, replica_groups=[list(range(NCORE))],
                                                                          ins=[tmp], outs=[dst]), [b], [], dma=9, inc=1)
        S.flush()

    def gather_lazy(self, names):
        S = self.S
        self.gw = Buf("gw")
        for n in names:
            src, tmp, dst = self.gsrc[n]
            b = Buf("g_" + n)
            self.dma(tmp, src, [], [b])
            tag = S.op("pool", lambda e, tmp=tmp, dst=dst: e.collective_compute("AllGather", ALU.bypass, replica_groups=[list(range(NCORE))],
                                                                                ins=[tmp], outs=[dst]), [b], [], dma=10, inc=1)
            self.gw.w = tag

    def phase_stage2(self):
        nc, S = self.nc, self.S
        B = self.bufs
        S.op("pool", lambda e: e.collective_compute("AllGather", ALU.bypass, replica_groups=[list(range(NCORE))],
                                                    ins=[self.u_d.rearrange("g c t -> (g c) t")],
                                                    outs=[self.u_all.rearrange("g c t -> (g c) t")]), [], [B["u_all"]], dma=9, inc=1)
        S.flush()
        nhalf = self.dbg.get("nhalf", 2)
        nfb = self.dbg.get("nfb", DFF // 256)
        TL = 512

        def mof(kt):
            return (kt // 2) * 4 + kt % 2 if kt < 16 else ((kt - 16) // 2) * 4 + 2 + kt % 2
        for hf in range(nhalf):
            A = Arena(nc)
            uT = A.sb("uT", [128, 32, TL], BF16)
            uxT = A.sb("uxT", [128, 32, TL], F32)
            ps = [A.ps("p2", [128, 512]) for _ in range(6)]
            pss = A.ps("pss", [128, 512])
            nps = 0
            x1b = [Buf("x1b%d" % i) for i in range(32)]

            pidc = {}

            def ld_u(e, r):
                if "v" not in pidc:
                    pidc["v"] = e.partition_id()
                pid = pidc["v"]
                src = self.u_all.rearrange("(r g) (q p) t -> p g r q t", g=NCORE, p=128)[:, bass.ds(pid, 1), r, :, hf * TL:(hf + 1) * TL]
                return e.dma_start(out=uT.t[:, r * 4:(r + 1) * 4, :].rearrange("p (o q) t -> p o q t", o=1), in_=src)
            for r in range(NCORE):
                S.op("pool", lambda e, r=r: ld_u(e, r), [B["u_all"]], [uT], dma=r % 6)
            wo = [A.sb("wo", [128, 32, 128], BF16) for _ in range(2)]
            sqt = [A.sb("sqt", [128, TL], F32) for _ in range(2)]
            shi = [A.sb("shi2", [128, TL], BF16) for _ in range(2)]
            slo = [A.sb("slo2", [128, TL], BF16) for _ in range(2)]
            rstd = A.sb("rstd", [128, TL], F32)
            tmp = [A.sb("tmp", [128, TL], F32) for _ in range(1)]
            x1t = [A.sb("x1t", [128, TL], F32) for _ in range(2)]
            xq = [A.sb("xq", [128, 4, 128], F32) for _ in range(1)]
            ot = [A.sb("ot", [128, 4, 128], F32) for _ in range(1)]
            for nt in range(32):
                w = wo[nt % 2]
                self.dma(w.t[:], self.w_out[:, nt * 128:(nt + 1) * 128].rearrange("(k p) n -> p k n", p=128), [self.gw], [w], q="pool", sem=6)
                p = ps[nps % 6]; nps += 1
                for kt in range(32):
                    self.mm(p.t[:], w.t[:, kt, :], uT.t[:, mof(kt), :], kt == 0, kt == 31, [w, uT], [p])
                self.act(uxT.t[:, nt, :], p.t[:], AF.Copy, [p], [uxT])
                sq = sqt[nt % 2]
                self.tt(sq.t[:], uxT.t[:, nt, :], uxT.t[:, nt, :], ALU.mult, [uxT], [sq], eng="pool")
                self.statmm(pss.t[:], pss, "ones", sq.t[:], sq, shi[nt % 2], slo[nt % 2], TL, start=(nt == 0), stop=(nt == 31))

            def mk_rstd():
                self.act(rstd.t[:], pss.t[:], AF.Sqrt, [pss, self.sm], [rstd], scale=1.0 / D, bias=self.SM(94))
                self.recip(rstd.t[:], rstd.t[:], [rstd], [rstd])
            mk_rstd()
            for nt in range(32):
                q = xq[0]
                self.dma(q.t[:], self.xs[hf * TL:(hf + 1) * TL, nt * 128:(nt + 1) * 128].rearrange("(b p) c -> p b c", p=128), [], [q])
                p = ps[nps % 6]; nps += 1
                for b in range(4):
                    self.tr(p.t[:, b * 128:(b + 1) * 128], q.t[:, b, :], self.C("ident"), [q, self.cst], [p], acc=(b != 0))
                t = tmp[0]
                self.stt(t.t[:], uxT.t[:, nt, :], self.MV(4, nt), rstd.t[:], ALU.mult, ALU.mult, [uxT, self.modv, rstd], [t])
                x1 = x1t[nt % 2]
                self.tt(x1.t[:], p.t[:], t.t[:], ALU.add, [p, t], [x1])
                sq = sqt[nt % 2]
                self.act(sq.t[:], x1.t[:], AF.Square, [x1], [sq])
                self.statmm(pss.t[:], pss, "ones", sq.t[:], sq, shi[nt % 2], slo[nt % 2], TL, start=(nt == 0), stop=(nt == 31))
                self.dma(self.x1_d[hf, nt], x1.t[:], [x1], [x1b[nt]])
            mk_rstd()
            hT = uT
            for nt in range(32):
                x1 = x1t[nt % 2]
                self.dma(x1.t[:], self.x1_d[hf, nt], [x1b[nt]], [x1])
                t = tmp[0]
                self.stt(t.t[:], x1.t[:], self.MV(5, nt), rstd.t[:], ALU.mult, ALU.mult, [x1, self.modv, rstd], [t])
                self.ts(hT.t[:, nt, :], t.t[:], self.MV(6, nt), None, ALU.add, None, [t, self.modv], [hT], eng="pool")
            yacc = uxT
            w1 = [A.sb("w1", [128, 32, 256], BF16) for _ in range(1)]
            w2 = [A.sb("w2", [128, 2, D], BF16) for _ in range(2)]
            aT = [A.sb("aT", [128, 2, TL], BF16) for _ in range(2)]
            rl = [A.sb("rl", [128, TL], F32) for _ in range(2)]
            for fb in range(nfb):
                a1, a2, a = w1[0], w2[fb % 2], aT[fb % 2]
                for kg in range(4):
                    self.dma(a1.t[:, kg * 8:(kg + 1) * 8, :],
                             self.w_ff1[kg * 1024:(kg + 1) * 1024, fb * 256:(fb + 1) * 256].rearrange("(k p) n -> p k n", p=128),
                             [self.gw], [a1], q="pool", sem=7)
                for h2 in range(2):
                    self.dma(a2.t[:, :, h2 * 2048:(h2 + 1) * 2048],
                             self.w_ff2[fb * 256:(fb + 1) * 256, h2 * 2048:(h2 + 1) * 2048].rearrange("(k p) n -> p k n", p=128),
                             [self.gw], [a2], q="pool", sem=8)
                for ft in range(2):
                    p = ps[nps % 6]; nps += 1
                    for kt in range(32):
                        self.mm(p.t[:], a1.t[:, kt, ft * 128:(ft + 1) * 128], hT.t[:, kt, :], kt == 0, kt == 31, [a1, hT], [p])
                    r_ = rl[ft]
                    self.act(r_.t[:], p.t[:], AF.Relu, [p], [r_])
                    self.tt(a.t[:, ft, :], r_.t[:], r_.t[:], ALU.mult, [r_], [a], eng="pool")
                for nt in range(32):
                    p = ps[nps % 6]; nps += 1
                    for ft in range(2):
                        self.mm(p.t[:], a2.t[:, ft, nt * 128:(nt + 1) * 128], a.t[:, ft, :], ft == 0, ft == 1, [a2, a], [p])
                    if fb == 0:
                        self.act(yacc.t[:, nt, :], p.t[:], AF.Copy, [p], [yacc])
                    else:
                        self.tt(yacc.t[:, nt, :], p.t[:], yacc.t[:, nt, :], ALU.add, [p, yacc], [yacc])
            for nt in range(32):
                sq = sqt[nt % 2]
                self.act(sq.t[:], yacc.t[:, nt, :], AF.Square, [yacc], [sq])
                self.statmm(pss.t[:], pss, "ones", sq.t[:], sq, shi[nt % 2], slo[nt % 2], TL, start=(nt == 0), stop=(nt == 31))
            mk_rstd()
            for nt in range(32):
                x1 = x1t[nt % 2]
                self.dma(x1.t[:], self.x1_d[hf, nt], [x1b[nt]], [x1])
                t = tmp[0]
                self.stt(t.t[:], yacc.t[:, nt, :], self.MV(7, nt), rstd.t[:], ALU.mult, ALU.mult, [yacc, self.modv, rstd], [t])
                self.tt(x1.t[:], x1.t[:], t.t[:], ALU.add, [x1, t], [x1], eng="pool")
                p = ps[nps % 6]; nps += 1
                for b in range(4):
                    self.tr(p.t[:, b * 128:(b + 1) * 128], x1.t[:, b * 128:(b + 1) * 128], self.C("ident"), [x1, self.cst], [p], acc=(b != 0))
                o = ot[0]
                self.act(o.t[:].rearrange("p b c -> p (b c)"), p.t[:], AF.Copy, [p], [o])
                self.dma(self.out[hf * TL:(hf + 1) * TL, nt * 128:(nt + 1) * 128].rearrange("(b p) c -> p b c", p=128), o.t[:], [o], [])
            S.flush()
            A.close()


def _build(dbg=None):
    k = K2(dbg)
    return k


def kernel(**inputs):
    maps = _prep(inputs)
    k = K2(dict(DEBUG))
    k.phase_gather(["xall"])
    k.phase0()
    k.gather_lazy(["w_out", "w_ff1", "w_ff2"])
    k.phase1()
    k.phase_shift()
    k.phase_rwkv()
    k.phase_hgrn()
    k.phase_out()
    k.phase_stage2()
    res = run_bass_kernel_spmd(k.nc, maps, core_ids=list(range(NCORE)))
    out = np.concatenate([np.asarray(r["out"], np.float32) for r in res.results], 0)
    return out[None]
```
